# Optimizing a Trainium2 kernel written in Bass

```python
import jax, jax.numpy as jnp
from jax import lax
import numpy as np

D_MODEL = 1024
BATCH = 32
SEQ = 2048
DEPTH = 2

D_MIX = 2 * D_MODEL
D_SSD = D_MIX // 2
SSD_HEAD_DIM = 64
SSD_HEADS = D_SSD // SSD_HEAD_DIM
SSD_GROUPS = 2
SSD_STATE = 128
SSD_CONV = 4
CHUNK = 128
MLA_HEADS = 8
QK_NOPE = 128
QK_ROPE = 64
V_DIM = 128
D_ATT = MLA_HEADS * V_DIM
Q_RANK = 384
KV_RANK = 256
ROPE_BASE = 10000.0
Q_BLOCK = 128
D_FF = 2816
FF_CONV = 3
EPS = 1e-6

D_XBC = D_SSD + 2 * SSD_GROUPS * SSD_STATE
IN_SIZES = [D_SSD, D_XBC, SSD_HEADS, Q_RANK, KV_RANK, QK_ROPE]
IN_SPLITS = np.cumsum(IN_SIZES)[:-1].tolist()
D_IN = int(sum(IN_SIZES))

kernel_name = "hybrid_ssd_mla_convglu_adaln"


def rms_norm(x, g):
    xf = x.astype(jnp.float32)
    y = xf * lax.rsqrt(jnp.mean(xf * xf, axis=-1, keepdims=True) + EPS)
    return y.astype(x.dtype) * g


def modulate(h, shift, scale):
    return h * (1 + scale[:, None, :]) + shift[:, None, :]


def causal_depthwise_conv(u, w, b):
    k = w.shape[0]
    out = lax.conv_general_dilated(
        u, w[:, None, :].astype(u.dtype), window_strides=(1,), padding=[(k - 1, 0)],
        dimension_numbers=('NWC', 'WIO', 'NWC'), feature_group_count=u.shape[-1])
    return out + b


def apply_rope(t, cos, sin):
    t1, t2 = jnp.split(t, 2, axis=-1)
    return jnp.concatenate([t1 * cos - t2 * sin, t2 * cos + t1 * sin], axis=-1)


def ssd_chunked_scan(xs, dt, a_head, bm, cm):
    bsz, s, h, p = xs.shape
    nc = s // CHUNK
    k = h // SSD_GROUPS
    x_dt = (xs * dt[..., None]).reshape(bsz, nc, CHUNK, SSD_GROUPS, k, p)
    a = (dt * a_head).reshape(bsz, nc, CHUNK, SSD_GROUPS, k).transpose(0, 1, 3, 4, 2)
    bm = bm.reshape(bsz, nc, CHUNK, SSD_GROUPS, SSD_STATE)
    cm = cm.reshape(bsz, nc, CHUNK, SSD_GROUPS, SSD_STATE)
    a_cs = jnp.cumsum(a, axis=-1)
    causal = jnp.asarray(np.tril(np.ones((CHUNK, CHUNK), dtype=bool)))
    seg = a_cs[..., :, None] - a_cs[..., None, :]
    decay_in = jnp.exp(jnp.where(causal, seg, -jnp.inf))
    cb = jnp.einsum('bclgn,bcsgn->bcgls', cm, bm)
    y_diag = jnp.einsum('bcgkls,bcsgkp->bclgkp', cb[:, :, :, None] * decay_in, x_dt)
    decay_to_end = jnp.exp(a_cs[..., -1:] - a_cs)
    chunk_states = jnp.einsum('bclgn,bcgkl,bclgkp->bcgkpn', bm, decay_to_end, x_dt)
    chunk_decay = jnp.exp(a_cs[..., -1])

    def step(state, inp):
        st, dec = inp
        return state * dec[..., None, None] + st, state

    init = jnp.zeros((bsz, SSD_GROUPS, k, p, SSD_STATE), chunk_states.dtype)
    _, prev = lax.scan(step, init, (jnp.moveaxis(chunk_states, 1, 0), jnp.moveaxis(chunk_decay, 1, 0)))
    prev = jnp.moveaxis(prev, 0, 1)
    y_off = jnp.einsum('bclgn,bcgkpn,bcgkl->bclgkp', cm, prev, jnp.exp(a_cs))
    return (y_diag + y_off).reshape(bsz, s, h, p)


def ssd_mixer(z, xbc, dt_raw, conv_w, conv_b, dt_bias, a_log, d_skip, ssd_norm):
    bsz, s, _ = z.shape
    xbc = jax.nn.silu(causal_depthwise_conv(xbc, conv_w, conv_b))
    xs, bm, cm = jnp.split(xbc, [D_SSD, D_SSD + SSD_GROUPS * SSD_STATE], axis=-1)
    xs = xs.reshape(bsz, s, SSD_HEADS, SSD_HEAD_DIM)
    bm = bm.reshape(bsz, s, SSD_GROUPS, SSD_STATE)
    cm = cm.reshape(bsz, s, SSD_GROUPS, SSD_STATE)
    dt = jax.nn.softplus(dt_raw + dt_bias)
    a_head = -jnp.exp(a_log)
    y = ssd_chunked_scan(xs, dt, a_head, bm, cm) + xs * d_skip[:, None]
    y = y.reshape(bsz, s, D_SSD) * jax.nn.silu(z)
    yg = y.reshape(bsz, s, SSD_GROUPS, D_SSD // SSD_GROUPS).astype(jnp.float32)
    yg = yg * lax.rsqrt(jnp.mean(yg * yg, axis=-1, keepdims=True) + EPS)
    return yg.reshape(bsz, s, D_SSD).astype(y.dtype) * ssd_norm


def mla_mixer(cq, ckv, k_rope_raw, cos, sin, q_norm, w_uq, kv_norm, w_ukv, attn_norm):
    bsz, s, _ = cq.shape
    q = (rms_norm(cq, q_norm) @ w_uq).reshape(bsz, s, MLA_HEADS, QK_NOPE + QK_ROPE)
    q_nope, q_rope = jnp.split(q, [QK_NOPE], axis=-1)
    q_rope = apply_rope(q_rope, cos[:, :, None, :], sin[:, :, None, :])
    kv = (rms_norm(ckv, kv_norm) @ w_ukv).reshape(bsz, s, MLA_HEADS, QK_NOPE + V_DIM)
    k_nope, v = jnp.split(kv, [QK_NOPE], axis=-1)
    k_rope = apply_rope(k_rope_raw, cos, sin)
    scale = (QK_NOPE + QK_ROPE) ** -0.5
    outs = []
    for i in range(s // Q_BLOCK):
        q0, q1 = i * Q_BLOCK, (i + 1) * Q_BLOCK
        sc = (jnp.einsum('bqhd,bkhd->bhqk', q_nope[:, q0:q1], k_nope[:, :q1])
              + jnp.einsum('bqhr,bkr->bhqk', q_rope[:, q0:q1], k_rope[:, :q1]))
        sc = sc.astype(jnp.float32) * scale
        mask = jnp.asarray(np.arange(q0, q1)[:, None] >= np.arange(q1)[None, :])
        probs = jax.nn.softmax(jnp.where(mask, sc, -jnp.inf), axis=-1).astype(v.dtype)
        outs.append(jnp.einsum('bhqk,bkhd->bqhd', probs, v[:, :q1]))
    o = jnp.concatenate(outs, axis=1).reshape(bsz, s, D_ATT)
    return rms_norm(o, attn_norm)


def conv_glu_ffn(h, w_up, conv_w, conv_b, w_down):
    u = causal_depthwise_conv(h @ w_up, conv_w, conv_b)
    gate, val = jnp.split(u, 2, axis=-1)
    return (jax.nn.silu(gate) * val) @ w_down


def setup_inputs(seed: int = 0) -> dict:
    key = jax.random.key(seed)
    ks = jax.random.split(key, 32)

    def nrm(k, shape, scale):
        return jax.random.normal(k, shape, jnp.float32) * scale

    def gain(k, shape):
        return 1.0 + nrm(k, shape, 0.02)

    L = DEPTH
    x = nrm(ks[0], (BATCH, SEQ, D_MODEL), 1.0)
    c = nrm(ks[1], (BATCH, D_MODEL), 1.0)
    offsets = jax.random.randint(ks[2], (BATCH, 1), 0, 4096, dtype=jnp.int32)
    positions = (offsets + jnp.arange(SEQ, dtype=jnp.int32)[None, :]).astype(jnp.int32)
    dt0 = jnp.exp(jax.random.uniform(ks[9], (L, SSD_HEADS), jnp.float32, np.log(1e-3), np.log(1e-1)))
    dt_bias = dt0 + jnp.log(-jnp.expm1(-dt0))
    a_log = jnp.log(jax.random.uniform(ks[10], (L, SSD_HEADS), jnp.float32, 1.0, 16.0))
    return {
        'x': x, 'c': c, 'positions': positions,
        'w_ada': nrm(ks[3], (L, D_MODEL, 6 * D_MODEL), D_MODEL ** -0.5),
        'b_ada': nrm(ks[4], (L, 6 * D_MODEL), 0.02),
        'norm_mix': gain(ks[5], (L, D_MODEL)),
        'w_in': nrm(ks[6], (L, D_MODEL, D_IN), D_MODEL ** -0.5),
        'conv_w': nrm(ks[7], (L, SSD_CONV, D_XBC), SSD_CONV ** -0.5),
        'conv_b': nrm(ks[8], (L, D_XBC), 0.02),
        'dt_bias': dt_bias,
        'a_log': a_log,
        'd_skip': gain(ks[11], (L, SSD_HEADS)),
        'ssd_norm': gain(ks[12], (L, D_SSD)),
        'q_norm': gain(ks[13], (L, Q_RANK)),
        'w_uq': nrm(ks[14], (L, Q_RANK, MLA_HEADS * (QK_NOPE + QK_ROPE)), Q_RANK ** -0.5),
        'kv_norm': gain(ks[15], (L, KV_RANK)),
        'w_ukv': nrm(ks[16], (L, KV_RANK, MLA_HEADS * (QK_NOPE + V_DIM)), KV_RANK ** -0.5),
        'attn_norm': gain(ks[17], (L, D_ATT)),
        'w_out': nrm(ks[18], (L, D_MIX, D_MODEL), D_MIX ** -0.5),
        'norm_mlp': gain(ks[19], (L, D_MODEL)),
        'w_up': nrm(ks[20], (L, D_MODEL, 2 * D_FF), D_MODEL ** -0.5),
        'conv_ff_w': nrm(ks[21], (L, FF_CONV, 2 * D_FF), FF_CONV ** -0.5),
        'conv_ff_b': nrm(ks[22], (L, 2 * D_FF), 0.02),
        'w_down': nrm(ks[23], (L, D_FF, D_MODEL), D_FF ** -0.5),
        'final_norm': gain(ks[24], (D_MODEL,)),
    }


def reference(x, c, positions, w_ada, b_ada, norm_mix, w_in, conv_w, conv_b, dt_bias, a_log,
              d_skip, ssd_norm, q_norm, w_uq, kv_norm, w_ukv, attn_norm, w_out, norm_mlp,
              w_up, conv_ff_w, conv_ff_b, w_down, final_norm):
    inv_freq = jnp.asarray(1.0 / (ROPE_BASE ** (np.arange(0, QK_ROPE, 2, dtype=np.float32) / QK_ROPE)))
    angles = positions.astype(jnp.float32)[..., None] * inv_freq
    cos = jnp.cos(angles).astype(x.dtype)
    sin = jnp.sin(angles).astype(x.dtype)
    c_act = jax.nn.silu(c)
    for l in range(DEPTH):
        mod = c_act @ w_ada[l] + b_ada[l]
        sh1, sc1, g1, sh2, sc2, g2 = jnp.split(mod, 6, axis=-1)
        h = modulate(rms_norm(x, norm_mix[l]), sh1, sc1)
        z, xbc, dt_raw, cq, ckv, kr = jnp.split(h @ w_in[l], IN_SPLITS, axis=-1)
        y_ssd = ssd_mixer(z, xbc, dt_raw, conv_w[l], conv_b[l], dt_bias[l], a_log[l], d_skip[l], ssd_norm[l])
        y_att = mla_mixer(cq, ckv, kr, cos, sin, q_norm[l], w_uq[l], kv_norm[l], w_ukv[l], attn_norm[l])
        y = jnp.concatenate([y_ssd, y_att], axis=-1) @ w_out[l]
        x = x + g1[:, None, :] * y
        h = modulate(rms_norm(x, norm_mlp[l]), sh2, sc2)
        x = x + g2[:, None, :] * conv_glu_ffn(h, w_up[l], conv_ff_w[l], conv_ff_b[l], w_down[l])
    return rms_norm(x, final_norm)
```

```python
import numpy as np
from contextlib import ExitStack
import concourse.bass as bass
import concourse.mybir as mybir
from concourse.bass_utils import run_bass_kernel_spmd

F32 = mybir.dt.float32
BF16 = mybir.dt.bfloat16
I32 = mybir.dt.int32
AF = mybir.ActivationFunctionType
ALU = mybir.AluOpType

D = 1024
SEQ = 2048
NCORES = 8
NBC = 4
DEPTH = 2
D_FF = 2816
EPS = 1e-6
NG = SEQ // 512
NCH = SEQ // 128
WIN_COLS = 3344
Z0 = 2304
SM_SCALE = 192.0 ** -0.5

PV = {}
_o = 0
for _n, _c in [("norm_mix", 8), ("conv_w", 48), ("conv_b", 12), ("q_norm", 3), ("kv_norm", 2),
               ("attn_norm", 8), ("ssd_norm", 8), ("norm_mlp", 8), ("ffw", 132), ("ffb", 44), ("b_ada", 48)]:
    PV[_n] = _o
    _o += _c
NPV = _o


class Res:
    __slots__ = ("name", "w", "r", "excl", "ds", "ep")

    def __init__(self, name, excl=False):
        self.name = name
        self.w = None
        self.r = []
        self.excl = excl
        self.ds = None
        self.ep = -1


class Eng:
    def __init__(self, nc, name, h):
        self.name = name
        self.h = h
        self.sem = nc.alloc_semaphore("sem_" + name)
        self.cnt = 0
        self.waited = {}
        self.pend = []


class Sched:
    def __init__(self, nc):
        self.nc = nc
        self.pe = Eng(nc, "pe", nc.tensor)
        self.act = Eng(nc, "act", nc.scalar)
        self.dve = Eng(nc, "dve", nc.vector)
        self.pool = Eng(nc, "pool", nc.gpsimd)
        self.sp = Eng(nc, "sp", nc.sync)
        self.engs = [self.pe, self.act, self.dve, self.pool, self.sp]
        self.nsem = 0
        self.dma_toks = []
        self.all_dsems = []
        self.free_dsems = []
        self.epoch = 0

    def res(self, name, excl=False):
        return Res(name, excl)

    def _wait(self, e, tok):
        sem, val, owner = tok
        key = id(sem)
        if e.waited.get(key, 0) >= val:
            return
        e.h.wait_ge(sem, val)
        e.waited[key] = val

    def _deps(self, e, reads, writes):
        toks = []
        for r in reads:
            if r.excl:
                if r.w is not None:
                    toks.append(r.w)
                toks.extend(r.r)
            elif r.w is not None:
                toks.append(r.w)
        for w in writes:
            if w.w is not None:
                toks.append(w.w)
            toks.extend(w.r)
        for t in toks:
            if t[2] is e and e is self.pe:
                continue
            self._wait(e, t)

    def _commit(self, tok, reads, writes):
        for r in reads:
            if r.excl:
                r.w = tok
                r.r = []
            else:
                r.r.append(tok)
                if len(r.r) > 48:
                    r.r = r.r[-48:]
        for w in writes:
            w.w = tok
            w.r = []

    def op(self, e, fn, reads=(), writes=(), inc=True):
        for o in self.engs:
            assert o is e or not o.pend, "pending non-inc ops on another engine"
        self._deps(e, reads, writes)
        ins = fn()
        if not inc:
            e.pend.append((reads, writes))
            return None
        e.cnt += 1
        ins.then_inc(e.sem, 1)
        tok = (e.sem, e.cnt, e)
        for (r, w) in e.pend:
            self._commit(tok, r, w)
        e.pend = []
        self._commit(tok, reads, writes)
        return tok

    def dma(self, q, out, in_, reads=(), writes=(), key=None):
        for o in self.engs:
            assert not o.pend
        self._deps(q, reads, writes)
        kr = key or (writes[0] if writes else reads[0])
        if kr.ds is None or kr.ep != self.epoch:
            if self.free_dsems:
                kr.ds = self.free_dsems.pop()
            else:
                kr.ds = [self.nc.alloc_semaphore("dsem%d" % self.nsem), 0]
                self.nsem += 1
                self.all_dsems.append(kr.ds)
            kr.ep = self.epoch
        ins = q.h.dma_start(out=out, in_=in_)
        kr.ds[1] += 16
        ins.then_inc(kr.ds[0], 16)
        tok = (kr.ds[0], kr.ds[1], None)
        self._commit(tok, reads, writes)
        self.dma_toks.append(tok)
        return tok

    def barrier(self):
        sp = self.sp
        last = {}
        for t in self.dma_toks:
            k = id(t[0])
            if k not in last or last[k][1] < t[1]:
                last[k] = t
        for t in last.values():
            self._wait(sp, t)
        self.dma_toks = []
        for o in self.engs:
            if o is not sp and o.cnt:
                self._wait(sp, (o.sem, o.cnt, o))
        sp.cnt += 1
        sp.h.sem_inc(sp.sem, 1)
        tok = (sp.sem, sp.cnt, sp)
        for e in self.engs:
            if e is not sp:
                self._wait(e, tok)
        self.epoch += 1
        self.free_dsems = list(self.all_dsems)


class Builder:
    def __init__(self, nb=NBC, depth=DEPTH, dbg=None):
        self.nb = nb
        self.depth = depth
        self.dbg = dbg or []
        self.dbg_out = {}
        nc = bass.Bass("TRN2", target_bir_lowering=False)
        self.nc = nc
        self.S = Sched(nc)
        S = self.S

        def din(name, shape, dt=F32):
            return nc.dram_tensor(name, list(shape), dt, kind="ExternalInput").ap()

        self.x_d = din("x", [nb, SEQ, D])
        self.cT_d = din("cT", [128, 8, NBC])
        self.pos_d = din("pos", [nb, SEQ], I32)
        self.wada_d = din("w_ada", [DEPTH, 128, 8, 6 * D])
        self.pvec_d = din("pvec", [128, DEPTH, NPV])
        self.hvec_d = din("hvec", [128, DEPTH, 48])
        self.fnorm_d = din("fnorm", [128, 8])
        self.win_d = din("w_in", [DEPTH, 128, 8, WIN_COLS])
        self.wuq_d = din("w_uq", [DEPTH, 128, 3, 2048])
        self.wukv_d = din("w_ukv", [DEPTH, 128, 2, 2048])
        self.wout_d = din("w_out", [DEPTH, 128, 16, D])
        self.wup_d = din("w_up", [DEPTH, 128, 8, 2 * D_FF])
        self.wdn_d = din("w_down", [DEPTH, 128, 22, D])
        self.cst_d = din("consts", [128, 4, 128])
        self.rope_d = din("ropec", [128, 2])
        self.out_d = nc.dram_tensor("out", [nb, SEQ, D], F32, kind="ExternalOutput").ap()
        self.xres_d = nc.dram_tensor("xres", [nb, 128, 8, SEQ], F32, kind="Internal").ap()
        self.Rxres = [[S.res("xres%d_%d" % (b, k)) for k in range(8)] for b in range(nb)]
        self.Rout = S.res("out")

        self.ps = nc.alloc_psum_tensor("ps", [128, 8, 512], F32).ap()
        self.PS = [S.res("ps%d" % i, excl=True) for i in range(8)]
        self.build()

    def sb(self, stack, name, shape, dt):
        self._uid = getattr(self, "_uid", 0) + 1
        h = stack.enter_context(self.nc.sbuf_tensor("%s_u%d" % (name, self._uid), list(shape), dt))
        return h.ap()

    def psf(self, b0, nb_):
        return self.ps[:, b0:b0 + nb_, :].rearrange("p a b -> p (a b)")

    def psb(self, b):
        return self.ps[:, b, :].bitcast(BF16)

    def dump(self, name, ap, res_list, shape, dt=F32):
        if name not in self.dbg:
            return
        S = self.S
        o = self.nc.dram_tensor("dbg_" + name, list(shape), dt, kind="ExternalOutput").ap()
        R = S.res("dbg_" + name)
        S.dma(S.sp, o, ap, reads=res_list, writes=[R])
        self.dbg_out[name] = R

    def build(self):
        nc, S = self.nc, self.S
        with ExitStack() as top:
            self.cst = self.sb(top, "cst", [128, 4, 128], F32)
            self.cstb = self.sb(top, "cstb", [128, 4, 128], BF16)
            self.ropec = self.sb(top, "ropec", [128, 2], F32)
            self.pvec = self.sb(top, "pvec", [128, DEPTH, NPV], F32)
            self.hvec = self.sb(top, "hvec", [128, DEPTH, 48], F32)
            self.fnorm = self.sb(top, "fnorm", [128, 8], F32)
            self.modT = self.sb(top, "modT", [128, DEPTH, 48, NBC], F32)
            self.ahead = self.sb(top, "ahead", [128, DEPTH, 16], F32)
            self.epsb = self.sb(top, "epsb", [128, 1], F32)
            S.op(S.dve, lambda: nc.vector.memset(self.epsb, EPS), writes=[S.res("eps")])
            self.Rc = S.res("consts")
            self.Rmod = S.res("modT")
            for dst, src in [(self.cst, self.cst_d), (self.ropec, self.rope_d), (self.pvec, self.pvec_d),
                             (self.hvec, self.hvec_d), (self.fnorm, self.fnorm_d)]:
                S.dma(S.sp, dst, src, writes=[self.Rc])
            S.op(S.dve, lambda: nc.vector.tensor_copy(self.cstb, self.cst), reads=[self.Rc], writes=[self.Rc])
            S.op(S.act, lambda: nc.scalar.activation(out=self.ahead, in_=self.hvec[:, :, 16:32], func=AF.Exp),
                 reads=[self.Rc], writes=[self.Rc])
            S.op(S.dve, lambda: nc.vector.tensor_scalar(out=self.ahead, in0=self.ahead, scalar1=-1.0, scalar2=None,
                                                         op0=ALU.mult), reads=[self.Rc], writes=[self.Rc])
            self.identf = self.cst[:, 0, :]
            self.Lf = self.cst[:, 1, :]
            self.Uf = self.cst[:, 2, :]
            self.onesf = self.cst[:, 3, :]
            self.identb = self.cstb[:, 0, :]
            self.Lb = self.cstb[:, 1, :]
            self.Ub = self.cstb[:, 2, :]
            self.negb = self.sb(top, "negb", [128, 128], BF16)
            S.op(S.dve, lambda: nc.vector.tensor_scalar(out=self.negb, in0=self.cst[:, 2, :], scalar1=-30000.0,
                                                         scalar2=None, op0=ALU.mult), reads=[self.Rc], writes=[self.Rc])
            self.onesb = self.cstb[:, 3, :]

            self.prologue_mod()
            S.barrier()
            for b in range(self.nb):
                with ExitStack() as bst:
                    self.cosT = self.sb(bst, "cosT", [128, SEQ], F32)
                    self.sinT = self.sb(bst, "sinT", [128, SEQ], F32)
                    self.Rrope = S.res("rope")
                    self.rope_tables(b)
                    self.load_x(b)
                    S.barrier()
                    for l in range(self.depth):
                        self.mixer(b, l)
                        self.ffn(b, l)
                    self.final(b)
                    S.barrier()
            S.barrier()

    def prologue_mod(self):
        nc, S = self.nc, self.S
        with ExitStack() as st:
            cT = self.sb(st, "cT", [128, 8, NBC], F32)
            cact = self.sb(st, "cact", [128, 8, NBC], BF16)
            wb = [self.sb(st, "wada%d" % i, [128, 8, 512], BF16) for i in range(2)]
            Rw = [S.res("wada%d" % i) for i in range(2)]
            Rct = S.res("cT")
            S.dma(S.sp, cT, self.cT_d, writes=[Rct])
            S.op(S.act, lambda: nc.scalar.activation(out=cact, in_=cT, func=AF.Silu), reads=[Rct], writes=[Rct])
            for l in range(self.depth):
                bank = l % 2
                for blk in range(12):
                    i = blk % 2
                    S.dma(S.pool, wb[i], self.wada_d[l, :, :, blk * 512:(blk + 1) * 512], writes=[Rw[i]])
                    for mt in range(4):
                        T = blk * 4 + mt
                        for kc in range(8):
                            S.op(S.pe, lambda: nc.tensor.matmul(self.ps[:, bank, T * NBC:(T + 1) * NBC],
                                                                wb[i][:, kc, mt * 128:(mt + 1) * 128], cact[:, kc, :],
                                                                start=(kc == 0), stop=(kc == 7)),
                                 reads=[Rw[i], Rct], writes=[self.PS[bank]], inc=(kc == 7))
                bada = self.pvec[:, l, PV["b_ada"]:PV["b_ada"] + 48]
                S.op(S.dve, lambda: nc.vector.tensor_tensor(
                    out=self.modT[:, l, :, :], in0=self.ps[:, bank, 0:48 * NBC].rearrange("p (t b) -> p t b", b=NBC),
                    in1=bada.unsqueeze(2).to_broadcast([128, 48, NBC]), op=ALU.add),
                    reads=[self.PS[bank], self.Rc], writes=[self.Rmod])
                for t0 in (8, 32):
                    S.op(S.dve, lambda: nc.vector.tensor_scalar(
                        out=self.modT[:, l, t0:t0 + 8, :], in0=self.modT[:, l, t0:t0 + 8, :], scalar1=1.0, scalar2=None,
                        op0=ALU.add), reads=[self.Rmod], writes=[self.Rmod])
            S.barrier()

    def rope_tables(self, b):
        nc, S = self.nc, self.S
        with ExitStack() as st:
            posi = self.sb(st, "posi", [128, SEQ], I32)
            ang = self.sb(st, "ang", [128, SEQ], F32)
            t1 = self.sb(st, "rt1", [128, SEQ], F32)
            t2 = self.sb(st, "rt2", [128, SEQ], F32)
            ki = self.sb(st, "rki", [128, SEQ], I32)
            R = S.res("ropetmp")
            S.dma(S.sp, posi, self.pos_d[b:b + 1, :].partition_broadcast(128), writes=[R])
            S.op(S.dve, lambda: nc.vector.tensor_copy(ang, posi), reads=[R], writes=[R])
            S.op(S.dve, lambda: nc.vector.tensor_scalar(out=ang, in0=ang, scalar1=self.ropec[:, 0:1], scalar2=None,
                                                         op0=ALU.mult), reads=[R, self.Rc], writes=[R])
            C1 = 6.28125
            C2 = float(2 * np.pi - 6.28125)
            for which, dst in ((0, self.sinT), (1, self.cosT)):
                if which == 1:
                    S.op(S.dve, lambda: nc.vector.tensor_scalar(out=t1, in0=ang, scalar1=float(np.pi / 2), scalar2=None,
                                                                 op0=ALU.add), reads=[R], writes=[R])
                    src = t1
                else:
                    S.op(S.dve, lambda: nc.vector.tensor_copy(t1, ang), reads=[R], writes=[R])
                    src = t1
                S.op(S.dve, lambda: nc.vector.tensor_scalar(out=ki, in0=src, scalar1=float(1.0 / (2 * np.pi)),
                                                             scalar2=None, op0=ALU.mult), reads=[R], writes=[R])
                S.op(S.dve, lambda: nc.vector.tensor_copy(t2, ki), reads=[R], writes=[R])
                S.op(S.dve, lambda: nc.vector.scalar_tensor_tensor(out=t1, in0=t2, scalar=-C1, in1=t1, op0=ALU.mult,
                                                                    op1=ALU.add), reads=[R], writes=[R])
                S.op(S.dve, lambda: nc.vector.scalar_tensor_tensor(out=t1, in0=t2, scalar=-C2, in1=t1, op0=ALU.mult,
                                                                    op1=ALU.add), reads=[R], writes=[R])
                S.op(S.dve, lambda: nc.vector.tensor_scalar(out=t2, in0=t1, scalar1=float(np.pi),
                                                             scalar2=float(-2 * np.pi), op0=ALU.is_gt, op1=ALU.mult),
                     reads=[R], writes=[R])
                S.op(S.dve, lambda: nc.vector.tensor_tensor(out=t1, in0=t1, in1=t2, op=ALU.add), reads=[R], writes=[R])
                S.op(S.dve, lambda: nc.vector.tensor_scalar(out=t1, in0=t1, scalar1=float(-np.pi), scalar2=float(np.pi),
                                                             op0=ALU.max, op1=ALU.min), reads=[R], writes=[R])
                if which == 0:
                    S.op(S.act, lambda: nc.scalar.activation(out=t2, in_=t1, func=AF.Sin), reads=[R], writes=[R])
                    S.op(S.dve, lambda: nc.vector.tensor_scalar(out=dst, in0=t2, scalar1=self.ropec[:, 1:2],
                                                                 scalar2=None, op0=ALU.mult),
                         reads=[R, self.Rc], writes=[self.Rrope])
                else:
                    S.op(S.act, lambda: nc.scalar.activation(out=dst, in_=t1, func=AF.Sin), reads=[R],
                         writes=[self.Rrope])
            S.barrier()

    def load_x(self, b):
        nc, S = self.nc, self.S
        with ExitStack() as st:
            xt = [[self.sb(st, "xtok%d_%d" % (i, t), [128, D], F32) for t in range(4)] for i in range(2)]
            Rxt = [[S.res("xtok%d_%d" % (i, t)) for t in range(4)] for i in range(2)]
            xg = [self.sb(st, "xTg%d" % i, [128, 8, 512], F32) for i in range(2)]
            Rxg = [[S.res("xTg%d_%d" % (i, hf)) for hf in range(2)] for i in range(2)]
            def ldx(g):
                for t in range(4):
                    r0 = (g * 4 + t) * 128
                    S.dma(S.sp, xt[g % 2][t], self.x_d[b, r0:r0 + 128, :], writes=[Rxt[g % 2][t]])

            ldx(0)
            for g in range(NG):
                i = g % 2
                if g + 1 < NG:
                    ldx(g + 1)
                for half in range(2):
                    for kq in range(4):
                        kc = half * 4 + kq
                        for t in range(4):
                            S.op(S.pe, lambda: nc.tensor.transpose(self.ps[:, kc, t * 128:(t + 1) * 128],
                                                                   xt[i][t][:, kc * 128:(kc + 1) * 128], self.identf),
                                 reads=[Rxt[i][t], self.Rc], writes=[self.PS[kc]], inc=(t == 3))
                    eng = S.act if half == 0 else S.dve
                    dst = xg[i][:, half * 4:(half + 1) * 4, :]
                    src = self.ps[:, half * 4:(half + 1) * 4, :]
                    if half == 0:
                        S.op(S.act, lambda: nc.scalar.copy(dst, src), reads=self.PS[0:4], writes=[Rxg[i][0]])
                    else:
                        S.op(S.dve, lambda: nc.vector.tensor_copy(dst, src), reads=self.PS[4:8], writes=[Rxg[i][1]])
                S.dma(S.sp, self.xres_d[b, :, :, g * 512:(g + 1) * 512], xg[i], reads=Rxg[i], writes=self.Rxres[b])
            S.barrier()

    def normmod(self, b, l, hT, RhT, normcol, sh0, sc0, st):
        nc, S = self.nc, self.S
        NB_ = 3
        xg = [self.sb(st, "nm_xg%d" % i, [128, 8, 512], F32) for i in range(NB_)]
        Rxg = [S.res("nm_xg%d" % i) for i in range(NB_)]
        sq = self.sb(st, "nm_sq", [128, 8, 512], BF16)
        Rsq = S.res("nm_sq")
        rstd = self.sb(st, "nm_rstd", [128, 512], F32)
        Rrstd = S.res("nm_rstd")
        A = self.sb(st, "nm_A", [128, 8], F32)
        RA = S.res("nm_A")
        RhTk = [S.res("nm_hTk%d" % kc) for kc in range(8)]
        S.op(S.dve, lambda: nc.vector.tensor_tensor(out=A, in0=self.pvec[:, l, normcol:normcol + 8],
                                                     in1=self.modT[:, l, sc0:sc0 + 8, b], op=ALU.mult),
             reads=[self.Rc, self.Rmod], writes=[RA])
        def ld(g):
            S.dma(S.sp, xg[g % NB_], self.xres_d[b, :, :, g * 512:(g + 1) * 512], reads=self.Rxres[b],
                  writes=[Rxg[g % NB_]])

        def stats(g):
            i = g % NB_
            S.op(S.act, lambda: nc.scalar.activation(out=sq, in_=xg[i], func=AF.Square), reads=[Rxg[i]], writes=[Rsq])
            for kc in range(8):
                S.op(S.pe, lambda: nc.tensor.matmul(self.ps[:, g % 2, :], self.onesb, sq[:, kc, :], start=(kc == 0),
                                                    stop=(kc == 7)), reads=[Rsq, self.Rc], writes=[self.PS[g % 2]],
                     inc=(kc == 7))

        ld(0)
        ld(1)
        ld(2)
        stats(0)
        for g in range(NG):
            i = g % NB_
            bank = g % 2
            S.op(S.act, lambda: nc.scalar.activation(out=rstd, in_=self.ps[:, bank, :], func=AF.Ln, scale=1.0 / D,
                                                     bias=self.epsb), reads=[self.PS[bank], self.Rc], writes=[Rrstd])
            S.op(S.act, lambda: nc.scalar.activation(out=rstd, in_=rstd, func=AF.Exp, scale=-0.5), reads=[Rrstd],
                 writes=[Rrstd])
            S.op(S.dve, lambda: nc.vector.tensor_tensor(out=xg[i], in0=xg[i],
                                                         in1=rstd.unsqueeze(1).to_broadcast([128, 8, 512]), op=ALU.mult),
                 reads=[Rrstd, Rxg[i]], writes=[Rxg[i]])
            if g + 1 < NG:
                stats(g + 1)
            for kc in range(8):
                o_ = hT[:, kc, g * 512:(g + 1) * 512]
                if kc % 2 == 0:
                    S.op(S.act, lambda: nc.scalar.activation(out=o_, in_=xg[i][:, kc, :], func=AF.Identity,
                                                             scale=A[:, kc:kc + 1],
                                                             bias=self.modT[:, l, sh0 + kc, b:b + 1]),
                         reads=[Rxg[i], RA, self.Rmod], writes=[RhTk[kc]])
                else:
                    S.op(S.dve, lambda: nc.vector.tensor_scalar(out=o_, in0=xg[i][:, kc, :], scalar1=A[:, kc:kc + 1],
                                                                 scalar2=self.modT[:, l, sh0 + kc, b:b + 1],
                                                                 op0=ALU.mult, op1=ALU.add),
                         reads=[Rxg[i], RA, self.Rmod], writes=[RhTk[kc]])
            if g + 3 < NG:
                ld(g + 3)

    def mixer(self, b, l):
        nc, S = self.nc, self.S
        with ExitStack() as mst:
            hT = self.sb(mst, "hT", [128, 8, SEQ], BF16)
            RhT = [S.res("hT%d" % c) for c in range(NCH)]
            cqn = self.sb(mst, "cqn", [128, 3, SEQ], BF16)
            ckvn = self.sb(mst, "ckvn", [128, 2, SEQ], BF16)
            krT = self.sb(mst, "krT", [128, SEQ], BF16)
            Rlat = S.res("lat")
            S.op(S.pool, lambda: nc.gpsimd.memset(krT[64:128, :], 0.0), writes=[Rlat])
            with ExitStack() as st:
                self.normmod(b, l, hT, S.res("hTall"), PV["norm_mix"], 0, 8, st)
                S.barrier()
            self.dump("hT", hT, [], [128, 8, SEQ], BF16)
            with ExitStack() as st:
                xbcT = self.sb(st, "xbcT", [128, 12, SEQ], BF16)
                Rxbc = S.res("xbcT")
                wz = self.sb(st, "wz", [128, 8, 1040], BF16)
                Rwz = S.res("wz")
                S.dma(S.pool, wz, self.win_d[l, :, :, Z0:Z0 + 1040], writes=[Rwz])
                with ExitStack() as st2:
                    self.in_proj(b, l, hT, xbcT, Rxbc, cqn, ckvn, krT, Rlat, st2)
                    S.barrier()
                self.dump("xbcT", xbcT, [], [128, 12, SEQ], BF16)
                self.dump("cqn", cqn, [], [128, 3, SEQ], BF16)
                self.dump("ckvn", ckvn, [], [128, 2, SEQ], BF16)
                self.dump("krT", krT[0:64, :], [], [64, SEQ], BF16)
                with ExitStack() as st2:
                    self.ssd(b, l, hT, RhT, xbcT, Rxbc, wz, Rwz, st2)
                    S.barrier()
            self.dump("yssdT", hT, [], [128, 8, SEQ], BF16)
            with ExitStack() as st:
                oT = self.sb(st, "oT", [128, 8, SEQ], BF16)
                RoT = S.res("oT")
                with ExitStack() as st2:
                    self.attention(b, l, cqn, ckvn, krT, Rlat, oT, RoT, st2)
                    S.barrier()
                self.dump("oT", oT, [], [128, 8, SEQ], BF16)
                with ExitStack() as st2:
                    self.out_proj(b, l, hT, oT, RoT, st2)
                    S.barrier()

    def in_proj(self, b, l, hT, xbcT, Rxbc, cqn, ckvn, krT, Rlat, st):
        nc, S = self.nc, self.S
        wblk = [self.sb(st, "wblk%d" % i, [128, 8, 512], BF16) for i in range(2)]
        Rw = [S.res("wblk%d" % i) for i in range(2)]
        acc = [self.sb(st, "acc%d" % i, [128, SEQ], F32) for i in range(2)]
        Racc = [S.res("acc%d" % i) for i in range(2)]
        RhT = S.res("hT_ro")
        cw = PV["conv_w"]
        nblk = 5
        qi = 0

        def load_blk(blk):
            c0 = blk * 512
            n = min(512, Z0 - c0)
            S.dma(S.pool, wblk[blk % 2][:, :, 0:n], self.win_d[l, :, :, c0:c0 + n], writes=[Rw[blk % 2]])

        def proj_tile(m, q):
            blk, mo = m // 4, (m % 4) * 128
            w = wblk[blk % 2]
            for g in range(NG):
                for kc in range(8):
                    S.op(S.pe, lambda: nc.tensor.matmul(self.ps[:, q * 4 + g, :], w[:, kc, mo:mo + 128],
                                                        hT[:, kc, g * 512:(g + 1) * 512], start=(kc == 0),
                                                        stop=(kc == 7)),
                         reads=[Rw[blk % 2], RhT], writes=[self.PS[q * 4 + g]], inc=(kc == 7))

        load_blk(0)
        for m in range(12):
            if m % 4 == 0 and m // 4 + 1 < nblk:
                load_blk(m // 4 + 1)
            q = m % 2
            proj_tile(m, q)
            pq = self.PS[q * 4:q * 4 + 4]
            pf = self.psf(q * 4, 4)
            a = acc[q]
            S.op(S.act, lambda: nc.scalar.activation(out=a, in_=pf, func=AF.Identity,
                                                     scale=self.pvec[:, l, cw + m * 4 + 3:cw + m * 4 + 4],
                                                     bias=self.pvec[:, l, PV["conv_b"] + m:PV["conv_b"] + m + 1]),
                 reads=pq + [self.Rc], writes=[Racc[q]])
            if m >= 1:
                S.op(S.act, lambda: nc.scalar.activation(out=xbcT[:, m - 1, :], in_=acc[1 - q], func=AF.Silu),
                     reads=[Racc[1 - q]], writes=[Rxbc])
            for k in range(3):
                sh = 3 - k
                S.op(S.dve, lambda: nc.vector.scalar_tensor_tensor(
                    out=a[:, sh:SEQ], in0=pf[:, 0:SEQ - sh], scalar=self.pvec[:, l, cw + m * 4 + k:cw + m * 4 + k + 1],
                    in1=a[:, sh:SEQ], op0=ALU.mult, op1=ALU.add), reads=pq + [self.Rc, Racc[q]], writes=[Racc[q]])
        S.op(S.act, lambda: nc.scalar.activation(out=xbcT[:, 11, :], in_=acc[1], func=AF.Silu), reads=[Racc[1]],
             writes=[Rxbc])
        sqh = [self.sb(st, "ip_sqh%d" % i, [128, 1024], BF16) for i in range(2)]
        Rsqh = [S.res("ip_sqh%d" % i) for i in range(2)]
        hq = 0
        pend_ss = None
        for (m0, nt, dst, ncol, dim) in ((12, 3, cqn, PV["q_norm"], 384), (15, 2, ckvn, PV["kv_norm"], 256)):
            for i in range(nt):
                m = m0 + i
                if m == 16:
                    load_blk(4)
                blk, mo = m // 4, (m % 4) * 128
                w = wblk[blk % 2]
                for hf in range(2):
                    hb = (hq % 2) * 2
                    for g2 in range(2):
                        g = hf * 2 + g2
                        for kc in range(8):
                            S.op(S.pe, lambda: nc.tensor.matmul(self.ps[:, hb + g2, :], w[:, kc, mo:mo + 128],
                                                                hT[:, kc, g * 512:(g + 1) * 512], start=(kc == 0),
                                                                stop=(kc == 7)),
                                 reads=[Rw[blk % 2], RhT], writes=[self.PS[hb + g2]], inc=(kc == 7))
                    pq = self.PS[hb:hb + 2]
                    pf = self.psf(hb, 2)
                    ts_ = slice(hf * 1024, (hf + 1) * 1024)
                    S.op(S.dve, lambda: nc.vector.tensor_copy(dst[:, i, ts_], pf), reads=pq, writes=[Rlat])
                    S.op(S.act, lambda: nc.scalar.activation(out=sqh[hq % 2], in_=pf, func=AF.Square), reads=pq,
                         writes=[Rsqh[hq % 2]])
                    if pend_ss is not None:
                        pend_ss()

                    def ss_mm(hf=hf, i=i, nt=nt, k=hq % 2):
                        for g2 in range(2):
                            g = hf * 2 + g2
                            S.op(S.pe, lambda: nc.tensor.matmul(self.ps[:, 4 + g, :], self.onesb,
                                                                sqh[k][:, g2 * 512:(g2 + 1) * 512],
                                                                start=(i == 0), stop=(i == nt - 1)),
                                 reads=[Rsqh[k], self.Rc], writes=[self.PS[4 + g]], inc=True)
                    pend_ss = ss_mm
                    hq += 1
            pend_ss()
            pend_ss = None
            S.op(S.act, lambda: nc.scalar.activation(out=acc[0], in_=self.psf(4, 4), func=AF.Ln, scale=1.0 / dim,
                                                     bias=self.epsb), reads=self.PS[4:8] + [self.Rc], writes=[Racc[0]])
            S.op(S.act, lambda: nc.scalar.activation(out=acc[0], in_=acc[0], func=AF.Exp, scale=-0.5),
                 reads=[Racc[0]], writes=[Racc[0]])
            for i in range(nt):
                S.op(S.dve, lambda: nc.vector.scalar_tensor_tensor(
                    out=dst[:, i, :], in0=dst[:, i, :], scalar=self.pvec[:, l, ncol + i:ncol + i + 1], in1=acc[0],
                    op0=ALU.mult, op1=ALU.mult), reads=[Rlat, Racc[0], self.Rc], writes=[Rlat])
        proj_tile(17, 0)
        pq = self.PS[0:4]
        pf = self.psf(0, 4)
        S.op(S.dve, lambda: nc.vector.tensor_tensor(out=acc[0][0:64, :], in0=pf[0:64, :], in1=self.cosT[0:64, :],
                                                     op=ALU.mult), reads=pq + [self.Rrope], writes=[Racc[0]])
        S.op(S.dve, lambda: nc.vector.tensor_tensor(out=acc[1][0:64, :], in0=pf[64:128, :], in1=self.sinT[64:128, :],
                                                     op=ALU.mult), reads=pq + [self.Rrope], writes=[Racc[1]])
        S.op(S.pool, lambda: nc.gpsimd.tensor_tensor(out=krT[0:64, :], in0=acc[0][0:64, :], in1=acc[1][0:64, :],
                                                     op=ALU.add), reads=[Racc[0], Racc[1]], writes=[Rlat])

    def ssd(self, b, l, hT, RhT, xbcT, Rxbc, wz, Rwz, st):
        nc, S = self.nc, self.S
        ps, PS = self.ps, self.PS

        def t2(name, shape, dt, n=2):
            return [(self.sb(st, "ssd_%s%d" % (name, i), shape, dt), S.res("ssd_%s%d" % (name, i))) for i in range(n)]

        def t1(name, shape, dt):
            return self.sb(st, "ssd_" + name, shape, dt), S.res("ssd_" + name)

        szs = t2("sz", [128, 1024], BF16)
        xss = t2("xs", [128, 16, 64], BF16)
        xdts = t2("xdt", [128, 16, 64], BF16)
        esegs = t2("eseg", [128, 4, 128], BF16, 4)
        MTs = t2("MT", [128, 4, 128], BF16, 4)
        dts = t2("dt", [128, 16], F32)
        as_ = t2("a", [128, 16], F32)
        acss = t2("acs", [128, 48], F32)
        Es = t2("E", [128, 16], F32)
        dtes = t2("dte", [128, 16], F32)
        cdecs = t2("cdec", [128, 16], F32)
        rshs = t2("rsh", [128, 16, 128], BF16)
        rsls = t2("rsl", [128, 16, 128], BF16)
        ahis = t2("ahi", [128, 16], BF16)
        alos = t2("alo", [128, 16], BF16)
        xde, Rxde = t1("xde", [128, 16, 64], BF16)
        Btok, RBtok = t1("Btok", [128, 256], BF16)
        cbm, Rcbm = t1("cbm", [128, 2, 128], BF16)
        yt, Ryt = t1("yt", [128, 16, 64], F32)
        y2, Ry2 = t1("y2", [128, 16, 64], F32)
        ssq, Rssq = t1("ssq", [128, 2], F32)
        junk, Rjunk = t1("junk", [128, 512], BF16)
        yh, Ryh = t1("yh", [128, 1024], BF16)
        state, Rstate = t1("state", [128, 16, 64], F32)
        prev, Rprev = t1("prev", [128, 1024], BF16)
        dtb = self.hvec[:, l, 0:16]
        dsk = self.hvec[:, l, 32:48]
        ah = self.ahead[:, l, :]
        G = [0, 1, 2, 3]
        KD, KZ, KX, KB = 0, [1, 2], 3, 4
        KO, KT, KS, KY = [4, 5], 5, [6, 7], [6, 7]

        S.op(S.dve, lambda: nc.vector.memset(state, 0.0), writes=[Rstate])
        S.op(S.dve, lambda: nc.vector.memset(prev, 0.0), writes=[Rprev])

        def stepP(c):
            s = c % 2
            cs = slice(c * 128, (c + 1) * 128)
            dt_, Rdt = dts[s]
            a_, Ra = as_[s]
            acs, Racs = acss[s]
            E_, RE = Es[s]
            dte, Rdte = dtes[s]
            cdec, Rcdec = cdecs[s]
            rsh, Rrsh = rshs[s]
            rsl, Rrsl = rsls[s]
            ahi, Rahi = ahis[s]
            alo, Ralo = alos[s]
            for kc in range(8):
                S.op(S.pe, lambda: nc.tensor.matmul(ps[:, KD, 0:16], hT[:, kc, cs], wz[:, kc, 1024:1040],
                                                    start=(kc == 0), stop=(kc == 7)),
                     reads=[RhT[c], Rwz], writes=[PS[KD]], inc=(kc == 7))
            S.op(S.dve, lambda: nc.vector.tensor_tensor(out=dt_, in0=ps[:, KD, 0:16], in1=dtb, op=ALU.add),
                 reads=[PS[KD], self.Rc], writes=[Rdt])
            S.op(S.act, lambda: nc.scalar.activation(out=dt_, in_=dt_, func=AF.Exp), reads=[Rdt], writes=[Rdt])
            S.op(S.act, lambda: nc.scalar.activation(out=dt_, in_=dt_, func=AF.Ln, bias=1.0), reads=[Rdt],
                 writes=[Rdt])
            S.op(S.dve, lambda: nc.vector.tensor_tensor(out=a_, in0=dt_, in1=ah, op=ALU.mult),
                 reads=[Rdt, self.Rc], writes=[Ra])
            S.op(S.dve, lambda: nc.vector.tensor_copy(ahi, a_), reads=[Ra], writes=[Rahi])
            S.op(S.dve, lambda: nc.vector.tensor_tensor(out=alo, in0=a_, in1=ahi, op=ALU.subtract),
                 reads=[Ra, Rahi], writes=[Ralo])
            S.op(S.pool, lambda: nc.gpsimd.tensor_tensor(out=rsh, in0=ahi.unsqueeze(2).to_broadcast([128, 16, 128]),
                                                         in1=self.Lb.unsqueeze(1).to_broadcast([128, 16, 128]),
                                                         op=ALU.mult), reads=[Rahi, self.Rc], writes=[Rrsh])
            S.op(S.pool, lambda: nc.gpsimd.tensor_tensor(out=rsl, in0=alo.unsqueeze(2).to_broadcast([128, 16, 128]),
                                                         in1=self.Lb.unsqueeze(1).to_broadcast([128, 16, 128]),
                                                         op=ALU.mult), reads=[Ralo, self.Rc], writes=[Rrsl])
            S.op(S.pe, lambda: nc.tensor.matmul(ps[:, KD, 16:32], self.Lf, a_, start=True, stop=True),
                 reads=[Ra, self.Rc], writes=[PS[KD]], inc=False)
            S.op(S.pe, lambda: nc.tensor.matmul(ps[:, KD, 32:48], self.onesf, a_, start=True, stop=True),
                 reads=[Ra, self.Rc], writes=[PS[KD]], inc=True)
            S.op(S.dve, lambda: nc.vector.tensor_copy(acs[:, 0:32], ps[:, KD, 16:48]), reads=[PS[KD]], writes=[Racs])
            S.op(S.act, lambda: nc.scalar.activation(out=E_, in_=acs[:, 0:16], func=AF.Exp), reads=[Racs], writes=[RE])
            S.op(S.dve, lambda: nc.vector.tensor_tensor(out=acs[:, 32:48], in0=acs[:, 16:32], in1=acs[:, 0:16],
                                                         op=ALU.subtract), reads=[Racs], writes=[Racs])
            S.op(S.act, lambda: nc.scalar.activation(out=dte, in_=acs[:, 32:48], func=AF.Exp), reads=[Racs],
                 writes=[Rdte])
            S.op(S.act, lambda: nc.scalar.activation(out=cdec, in_=acs[:, 16:32], func=AF.Exp), reads=[Racs],
                 writes=[Rcdec])

        def stepA(c):
            s = c % 2
            rsh, Rrsh = rshs[s]
            rsl, Rrsl = rsls[s]
            for q in range(4):
                S.op(S.pe, lambda: nc.tensor.matmul(ps[:, G[q], :], self.Ub,
                                                    rsh[:, q * 4:q * 4 + 4, :].rearrange("p h l -> p (h l)"),
                                                    start=True, stop=False),
                     reads=[Rrsh, self.Rc], writes=[PS[G[q]]], inc=False)
                S.op(S.pe, lambda: nc.tensor.matmul(ps[:, G[q], :], self.Ub,
                                                    rsl[:, q * 4:q * 4 + 4, :].rearrange("p h l -> p (h l)"),
                                                    start=False, stop=True),
                     reads=[Rrsl, self.Rc], writes=[PS[G[q]]], inc=True)
            for q in range(4):
                eseg, Reseg = esegs[q]
                MT, RMT = MTs[q]
                S.op(S.act, lambda: nc.scalar.activation(out=eseg.rearrange("p h l -> p (h l)"), in_=ps[:, G[q], :],
                                                         func=AF.Exp), reads=[PS[G[q]]], writes=[Reseg])
                S.op(S.dve, lambda: nc.vector.tensor_tensor(out=MT, in0=eseg,
                                                             in1=cbm[:, q // 2:q // 2 + 1, :].to_broadcast([128, 4, 128]),
                                                             op=ALU.mult), reads=[Reseg, Rcbm], writes=[RMT])

        def stepB(c):
            s = c % 2
            cs = slice(c * 128, (c + 1) * 128)
            E_, RE = Es[s]
            cdec, Rcdec = cdecs[s]
            for hh in range(2):
                S.op(S.pe, lambda: nc.tensor.matmul(ps[:, KO[hh], :], xbcT[:, 10 + hh, cs],
                                                    prev[:, hh * 512:(hh + 1) * 512], start=True, stop=True),
                     reads=[Rxbc, Rprev], writes=[PS[KO[hh]]], inc=True)
            for g in range(2):
                S.op(S.pe, lambda: nc.tensor.matmul(ps[:, KS[g], :], Btok[:, g * 128:(g + 1) * 128],
                                                    xde[:, g * 8:(g + 1) * 8, :].rearrange("p h d -> p (h d)"),
                                                    start=True, stop=True),
                     reads=[RBtok, Rxde], writes=[PS[KS[g]]], inc=True)
            for hh in range(2):
                hs = slice(hh * 8, (hh + 1) * 8)
                S.op(S.dve, lambda: nc.vector.tensor_tensor(out=yt[:, hs, :],
                                                             in0=ps[:, KO[hh], :].rearrange("p (h d) -> p h d", h=8),
                                                             in1=E_[:, hs].unsqueeze(2).to_broadcast([128, 8, 64]),
                                                             op=ALU.mult), reads=[PS[KO[hh]], RE], writes=[Ryt])
            S.op(S.pool, lambda: nc.gpsimd.tensor_tensor(out=state, in0=state,
                                                         in1=cdec.unsqueeze(2).to_broadcast([128, 16, 64]), op=ALU.mult),
                 reads=[Rstate, Rcdec], writes=[Rstate])
            for g in range(2):
                hs = slice(g * 8, (g + 1) * 8)
                S.op(S.dve, lambda: nc.vector.tensor_tensor(out=state[:, hs, :],
                                                             in0=ps[:, KS[g], :].rearrange("p (h d) -> p h d", h=8),
                                                             in1=state[:, hs, :], op=ALU.add),
                     reads=[PS[KS[g]], Rstate], writes=[Rstate])

        def stepC(c):
            s = c % 2
            cs = slice(c * 128, (c + 1) * 128)
            Rh = RhT[c]
            sz, Rsz = szs[s]
            xs, Rxs = xss[s]
            xdt, Rxdt = xdts[s]
            dt_, Rdt = dts[s]
            dte, Rdte = dtes[s]
            for n in range(2):
                for kc in range(8):
                    S.op(S.pe, lambda: nc.tensor.matmul(ps[:, KZ[n], :], hT[:, kc, cs], wz[:, kc, n * 512:(n + 1) * 512],
                                                        start=(kc == 0), stop=(kc == 7)),
                         reads=[Rh, Rwz], writes=[PS[KZ[n]]], inc=(kc == 7))
            for k in range(8):
                S.op(S.pe, lambda: nc.tensor.transpose(self.psb(KX)[:, k * 128:(k + 1) * 128], xbcT[:, k, cs],
                                                       self.identb), reads=[Rxbc, self.Rc], writes=[PS[KX]],
                     inc=(k == 7))
            for k in range(2):
                S.op(S.pe, lambda: nc.tensor.transpose(self.psb(KB)[:, k * 128:(k + 1) * 128], xbcT[:, 8 + k, cs],
                                                       self.identb), reads=[Rxbc, self.Rc], writes=[PS[KB]], inc=False)
            for g in range(2):
                S.op(S.pe, lambda: nc.tensor.matmul(ps[:, KB, 128 + g * 128:256 + g * 128], xbcT[:, 8 + g, cs],
                                                    xbcT[:, 10 + g, cs], start=True, stop=True),
                     reads=[Rxbc], writes=[PS[KB]], inc=(g == 1))
            for n in range(2):
                S.op(S.act, lambda: nc.scalar.activation(out=sz[:, n * 512:(n + 1) * 512], in_=ps[:, KZ[n], :],
                                                         func=AF.Silu), reads=[PS[KZ[n]]], writes=[Rsz])
            xsp = self.psb(KX).rearrange("p (h d) -> p h d", h=16)
            S.op(S.act, lambda: nc.scalar.copy(xs, xsp), reads=[PS[KX]], writes=[Rxs])
            S.op(S.dve, lambda: nc.vector.tensor_tensor(out=xdt, in0=xsp, in1=dt_.unsqueeze(2).to_broadcast([128, 16, 64]),
                                                         op=ALU.mult), reads=[PS[KX], Rdt], writes=[Rxdt])
            S.op(S.act, lambda: nc.scalar.copy(Btok, self.psb(KB)[:, 0:256]), reads=[PS[KB]], writes=[RBtok])
            S.op(S.dve, lambda: nc.vector.tensor_tensor(out=cbm, in0=ps[:, KB, 128:384].rearrange("p (g l) -> p g l", g=2),
                                                         in1=self.Lf.unsqueeze(1).to_broadcast([128, 2, 128]),
                                                         op=ALU.mult), reads=[PS[KB], self.Rc], writes=[Rcbm])
            S.op(S.dve, lambda: nc.vector.tensor_tensor(out=xde, in0=xdt,
                                                         in1=dte.unsqueeze(2).to_broadcast([128, 16, 64]), op=ALU.mult),
                 reads=[Rxdt, Rdte], writes=[Rxde])

        def stepD(c):
            cs = slice(c * 128, (c + 1) * 128)
            for k in range(8):
                S.op(S.pe, lambda: nc.tensor.transpose(self.psb(KT)[:, k * 128:(k + 1) * 128], yh[:, k * 128:(k + 1) * 128],
                                                       self.identb), reads=[Ryh, self.Rc], writes=[PS[KT]], inc=(k == 7))
            S.op(S.dve, lambda: nc.vector.tensor_copy(hT[:, :, cs], self.psb(KT).rearrange("p (k t) -> p k t", k=8)),
                 reads=[PS[KT]], writes=[RhT[c]])

        def stepE1(c):
            s = c % 2
            sz, Rsz = szs[s]
            xs, Rxs = xss[s]
            xdt, Rxdt = xdts[s]
            for hh in range(2):
                for jj in range(8):
                    MT, RMT = MTs[hh * 2 + jj // 4]
                    S.op(S.pe, lambda: nc.tensor.matmul(ps[:, KY[hh], jj * 64:(jj + 1) * 64], MT[:, jj % 4, :],
                                                        xdt[:, hh * 8 + jj, :], start=True, stop=True),
                         reads=[RMT, Rxdt], writes=[PS[KY[hh]]], inc=(jj == 7))
            S.op(S.dve, lambda: nc.vector.tensor_tensor(out=y2, in0=xs, in1=dsk.unsqueeze(2).to_broadcast([128, 16, 64]),
                                                         op=ALU.mult), reads=[Rxs, self.Rc], writes=[Ry2])
            for hh in range(2):
                hs = slice(hh * 8, (hh + 1) * 8)
                S.op(S.dve, lambda: nc.vector.tensor_tensor(out=yt[:, hs, :],
                                                             in0=ps[:, KY[hh], :].rearrange("p (h d) -> p h d", h=8),
                                                             in1=yt[:, hs, :], op=ALU.add),
                     reads=[PS[KY[hh]], Ryt], writes=[Ryt])
            S.op(S.dve, lambda: nc.vector.tensor_tensor(out=y2, in0=y2, in1=yt, op=ALU.add), reads=[Ry2, Ryt],
                 writes=[Ry2])
            y2f = y2.rearrange("p h d -> p (h d)")
            S.op(S.dve, lambda: nc.vector.tensor_tensor(out=y2f, in0=y2f, in1=sz, op=ALU.mult), reads=[Ry2, Rsz],
                 writes=[Ry2])
            S.op(S.act, lambda: nc.scalar.copy(prev, state.rearrange("p h d -> p (h d)")), reads=[Rstate],
                 writes=[Rprev])

        def stepE2(c):
            y2f = y2.rearrange("p h d -> p (h d)")
            for g in range(2):
                S.op(S.act, lambda: nc.scalar.activation(out=junk, in_=y2f[:, g * 512:(g + 1) * 512], func=AF.Square,
                                                         accum_out=ssq[:, g:g + 1]), reads=[Ry2],
                     writes=[Rjunk, Rssq])
            S.op(S.act, lambda: nc.scalar.activation(out=ssq, in_=ssq, func=AF.Ln, scale=1.0 / 512, bias=self.epsb),
                 reads=[Rssq, self.Rc], writes=[Rssq])
            S.op(S.act, lambda: nc.scalar.activation(out=ssq, in_=ssq, func=AF.Exp, scale=-0.5),
                 reads=[Rssq], writes=[Rssq])
            for g in range(2):
                S.op(S.act, lambda: nc.scalar.activation(out=yh[:, g * 512:(g + 1) * 512],
                                                         in_=y2f[:, g * 512:(g + 1) * 512], func=AF.Copy,
                                                         scale=ssq[:, g:g + 1]), reads=[Ry2, Rssq], writes=[Ryh])

        stepP(0)
        stepP(1)
        stepC(0)
        for c in range(NCH):
            stepA(c)
            if c >= 1:
                stepE2(c - 1)
            stepB(c)
            if c + 2 < NCH:
                stepP(c + 2)
            if c + 1 < NCH:
                stepC(c + 1)
            if c >= 1:
                stepD(c - 1)
            stepE1(c)
        stepE2(NCH - 1)
        stepD(NCH - 1)

    def attention(self, b, l, cqn, ckvn, krT, Rlat, oT, RoT, st):
        nc, S = self.nc, self.S
        ps, PS = self.ps, self.PS
        vall = self.sb(st, "vall", [128, NCH, 1024], BF16)
        Rv = [S.res("vall0"), S.res("vall1")]
        if True:
            wv = self.sb(st, "wv", [128, 2, 1024], BF16)
            Rwv = S.res("wv")
            S.dma(S.pool, wv, self.wukv_d[l, :, :, 1024:2048], writes=[Rwv])
            for tkn in range(NCH):
                for n in range(2):
                    bank = (tkn * 2 + n) % 4
                    for kc in range(2):
                        S.op(S.pe, lambda: nc.tensor.matmul(ps[:, bank, :], ckvn[:, kc, tkn * 128:(tkn + 1) * 128],
                                                            wv[:, kc, n * 512:(n + 1) * 512],
                                                            start=(kc == 0), stop=(kc == 1)),
                             reads=[Rlat, Rwv], writes=[PS[bank]], inc=(kc == 1))
                    if n == 0:
                        S.op(S.act, lambda: nc.scalar.copy(vall[:, tkn, 0:512], ps[:, bank, :]), reads=[PS[bank]],
                             writes=[Rv[0]])
                    else:
                        S.op(S.dve, lambda: nc.vector.tensor_copy(vall[:, tkn, 512:1024], ps[:, bank, :]),
                             reads=[PS[bank]], writes=[Rv[1]])
        wqh = [self.sb(st, "wqh%d" % i, [128, 3, 256], BF16) for i in range(2)]
        wkh = [self.sb(st, "wkh%d" % i, [128, 2, 128], BF16) for i in range(2)]
        Rwh = [S.res("wh%d" % i) for i in range(2)]
        qn = [self.sb(st, "qn%d" % i, [128, SEQ], BF16) for i in range(2)]
        kn = [self.sb(st, "kn%d" % i, [128, SEQ], BF16) for i in range(2)]
        qr = [self.sb(st, "qr%d" % i, [128, SEQ], BF16) for i in range(2)]
        Rqn = [S.res("qn%d" % i) for i in range(2)]
        Rkn = [S.res("kn%d" % i) for i in range(2)]
        Rqr = [S.res("qr%d" % i) for i in range(2)]
        for i in range(2):
            S.op(S.pool, lambda: nc.gpsimd.memset(qr[i][64:128, :], 0.0), writes=[Rqr[i]])
        r1 = self.sb(st, "r1", [64, 512], F32)
        r2 = self.sb(st, "r2", [64, 512], F32)
        Rr1, Rr2 = S.res("r1"), S.res("r2")
        NP = 5
        PT = [self.sb(st, "PT%d" % i, [128, 512], BF16) for i in range(NP)]
        RPT = [S.res("PT%d" % i) for i in range(NP)]
        rs = self.sb(st, "rs", [128, 512], F32)
        Rrs = S.res("rs")

        def load_w(h):
            i = h % 2
            S.dma(S.pool, wqh[i], self.wuq_d[l, :, :, h * 256:(h + 1) * 256], writes=[Rwh[i]])
            S.dma(S.pool, wkh[i], self.wukv_d[l, :, :, h * 128:(h + 1) * 128], writes=[Rwh[i]])

        def proj_pieces(h):
            i = h % 2
            pieces = []
            for g in range(NG):
                gs = slice(g * 512, (g + 1) * 512)

                def p_qn(gs=gs):
                    for kc in range(3):
                        S.op(S.pe, lambda: nc.tensor.matmul(ps[:, 0, :], wqh[i][:, kc, 0:128], cqn[:, kc, gs],
                                                            start=(kc == 0), stop=(kc == 2)),
                             reads=[Rlat, Rwh[i]], writes=[PS[0]], inc=(kc == 2))
                    S.op(S.act, lambda: nc.scalar.copy(qn[i][:, gs], ps[:, 0, :]), reads=[PS[0]], writes=[Rqn[i]])

                def p_qr(gs=gs):
                    for kc in range(3):
                        S.op(S.pe, lambda: nc.tensor.matmul(ps[:, 0, :], wqh[i][:, kc, 128:256], cqn[:, kc, gs],
                                                            start=(kc == 0), stop=(kc == 2)),
                             reads=[Rlat, Rwh[i]], writes=[PS[0]], inc=(kc == 2))
                    S.op(S.dve, lambda: nc.vector.tensor_tensor(out=r1, in0=ps[0:64, 0, :], in1=self.cosT[0:64, gs],
                                                                 op=ALU.mult), reads=[PS[0], self.Rrope], writes=[Rr1])
                    S.op(S.dve, lambda: nc.vector.tensor_tensor(out=r2, in0=ps[64:128, 0, :], in1=self.sinT[64:128, gs],
                                                                 op=ALU.mult), reads=[PS[0], self.Rrope], writes=[Rr2])
                    S.op(S.dve, lambda: nc.vector.tensor_tensor(out=qr[i][0:64, gs], in0=r1, in1=r2, op=ALU.add),
                         reads=[Rr1, Rr2], writes=[Rqr[i]])

                def p_kn(gs=gs):
                    for kc in range(2):
                        S.op(S.pe, lambda: nc.tensor.matmul(ps[:, 0, :], wkh[i][:, kc, :], ckvn[:, kc, gs],
                                                            start=(kc == 0), stop=(kc == 1)),
                             reads=[Rlat, Rwh[i]], writes=[PS[0]], inc=(kc == 1))
                    S.op(S.act, lambda: nc.scalar.copy(kn[i][:, gs], ps[:, 0, :]), reads=[PS[0]], writes=[Rkn[i]])

                pieces += [p_qn, p_qr, p_kn]
            return pieces

        load_w(0)
        load_w(1)
        for p in proj_pieces(0):
            p()
        pi = 0
        sc = 0
        for h in range(8):
            i = h % 2
            pieces = proj_pieces(h + 1) if h + 1 < 8 else []
            ucount = 0
            for j in range(NG):
                ob, sb_ = 4 + (j % 2) * 2, 5 + (j % 2) * 2
                units = list(range(4 * j + 4))
                pend = []
                for ui, kt in enumerate(units):
                    off = max(0, kt - 4 * j) * 128
                    qs = slice(j * 512 + off, (j + 1) * 512)
                    ks = slice(kt * 128, (kt + 1) * 128)
                    sbank = 1 + sc % 3
                    sc += 1
                    diag = kt >= 4 * j
                    S.op(S.pe, lambda: nc.tensor.matmul(ps[:, sbank, off:512], kn[i][:, ks], qn[i][:, qs], start=True,
                                                        stop=False), reads=[Rkn[i], Rqn[i]], writes=[PS[sbank]],
                         inc=False)
                    S.op(S.pe, lambda: nc.tensor.matmul(ps[:, sbank, off:512], krT[:, ks], qr[i][:, qs], start=False,
                                                        stop=not diag), reads=[Rlat, Rqr[i]], writes=[PS[sbank]],
                         inc=not diag)
                    if diag:
                        S.op(S.pe, lambda: nc.tensor.matmul(ps[:, sbank, off:off + 128], self.identb, self.negb,
                                                            start=False, stop=True), reads=[self.Rc],
                             writes=[PS[sbank]], inc=True)
                    if len(pend) >= 2:
                        self._pv(*pend.pop(0))
                    p = pi % NP
                    pi += 1
                    S.op(S.act, lambda: nc.scalar.activation(out=PT[p][:, off:512], in_=ps[:, sbank, off:512],
                                                             func=AF.Exp, scale=SM_SCALE), reads=[PS[sbank]],
                         writes=[RPT[p]])
                    pend.append((vall, Rv, h, kt, off, PT[p], RPT[p], ob, sb_, ui == 0, ui == len(units) - 1))
                    ucount += 1
                    if pieces and ucount % 3 == 0:
                        pieces.pop(0)()
                while pend:
                    self._pv(*pend.pop(0))
                S.op(S.dve, lambda: nc.vector.reciprocal(rs, ps[:, sb_, :]), reads=[PS[sb_]], writes=[Rrs])
                S.op(S.dve, lambda: nc.vector.tensor_tensor(out=oT[:, h, j * 512:(j + 1) * 512], in0=ps[:, ob, :], in1=rs,
                                                             op=ALU.mult), reads=[PS[ob], Rrs], writes=[RoT])
            while pieces:
                pieces.pop(0)()
            if h + 2 < 8:
                load_w(h + 2)

    def _pv(self, vall, Rv, h, kt, off, PTp, RPTp, ob, sb_, first, last):
        nc, S = self.nc, self.S
        ps, PS = self.ps, self.PS
        S.op(S.pe, lambda: nc.tensor.matmul(ps[:, ob, off:512], vall[:, kt, h * 128:(h + 1) * 128], PTp[:, off:512],
                                            start=first, stop=last), reads=Rv + [RPTp], writes=[PS[ob]], inc=False)
        S.op(S.pe, lambda: nc.tensor.matmul(ps[:, sb_, off:512], self.onesb, PTp[:, off:512], start=first, stop=last),
             reads=[self.Rc, RPTp], writes=[PS[sb_]], inc=True)

    def out_proj(self, b, l, yT, oT, RoT, st):
        nc, S = self.nc, self.S
        ps, PS = self.ps, self.PS
        NQ = 2
        wo = [self.sb(st, "wo%d" % i, [128, 16, 512], BF16) for i in range(NQ)]
        Rwo = [S.res("wo%d" % i) for i in range(NQ)]
        for i in range(NQ):
            S.dma(S.pool, wo[i], self.wout_d[l, :, :, i * 512:(i + 1) * 512], writes=[Rwo[i]])
        sq = self.sb(st, "op_sq", [128, 8, 512], BF16)
        Rsq = S.res("op_sq")
        rstd = self.sb(st, "op_rstd", [128, 512], F32)
        Rrstd = S.res("op_rstd")
        RyT = S.res("yT_ro")
        def stats(g):
            gs = slice(g * 512, (g + 1) * 512)
            S.op(S.act, lambda: nc.scalar.activation(out=sq, in_=oT[:, :, gs], func=AF.Square), reads=[RoT],
                 writes=[Rsq])
            for kc in range(8):
                S.op(S.pe, lambda: nc.tensor.matmul(ps[:, g % 2, :], self.onesb, sq[:, kc, :], start=(kc == 0),
                                                    stop=(kc == 7)), reads=[Rsq, self.Rc], writes=[PS[g % 2]],
                     inc=(kc == 7))

        stats(0)
        for g in range(NG):
            gs = slice(g * 512, (g + 1) * 512)
            bank = g % 2
            S.op(S.act, lambda: nc.scalar.activation(out=rstd, in_=ps[:, bank, :], func=AF.Ln, scale=1.0 / D,
                                                     bias=self.epsb), reads=[PS[bank], self.Rc], writes=[Rrstd])
            S.op(S.act, lambda: nc.scalar.activation(out=rstd, in_=rstd, func=AF.Exp, scale=-0.5), reads=[Rrstd],
                 writes=[Rrstd])
            if g + 1 < NG:
                stats(g + 1)
            S.op(S.dve, lambda: nc.vector.tensor_tensor(out=oT[:, :, gs], in0=oT[:, :, gs],
                                                         in1=rstd.unsqueeze(1).to_broadcast([128, 8, 512]), op=ALU.mult),
                 reads=[RoT, Rrstd], writes=[RoT])
        for i in range(NQ):
            for kc in range(16):
                col = (PV["ssd_norm"] + kc) if kc < 8 else (PV["attn_norm"] + kc - 8)
                S.op(S.dve, lambda: nc.vector.tensor_scalar(out=wo[i][:, kc, :], in0=wo[i][:, kc, :],
                                                             scalar1=self.pvec[:, l, col:col + 1], scalar2=None,
                                                             op0=ALU.mult),
                     reads=[Rwo[i], self.Rc], writes=[Rwo[i]])
        self.dump("oTn", oT, [RoT], [128, 8, SEQ], BF16)
        self.resid_proj(b, l, lambda kc: (yT[:, kc, :] if kc < 8 else oT[:, kc - 8, :]), [RyT, RoT], 16, wo, Rwo, 16, st)

    def resid_proj(self, b, l, src, Rsrc, nk, w, Rw, gate0, st):
        nc, S = self.nc, self.S
        ps, PS = self.ps, self.PS
        xm = [self.sb(st, "rp_xm%d" % i, [128, SEQ], F32) for i in range(2)]
        Rxm = [S.res("rp_xm%d" % i) for i in range(2)]

        def ld(m):
            S.dma(S.sp, xm[m % 2], self.xres_d[b, :, m, :], reads=[self.Rxres[b][m]], writes=[Rxm[m % 2]])

        ld(0)
        bi = 0
        for m in range(8):
            if m + 1 < 8:
                ld(m + 1)
            per = 8 // len(w)
            wi, mo = m // per, (m % per) * 128
            for g in range(NG):
                bank = 4 + bi % 4
                bi += 1
                for kc in range(nk):
                    S.op(S.pe, lambda: nc.tensor.matmul(ps[:, bank, :], w[wi][:, kc, mo:mo + 128],
                                                        src(kc)[:, g * 512:(g + 1) * 512], start=(kc == 0),
                                                        stop=(kc == nk - 1)),
                         reads=[Rw[wi]] + Rsrc, writes=[PS[bank]], inc=(kc == nk - 1))
                S.op(S.dve, lambda: nc.vector.scalar_tensor_tensor(
                    out=xm[m % 2][:, g * 512:(g + 1) * 512], in0=ps[:, bank, :],
                    scalar=self.modT[:, l, gate0 + m, b:b + 1], in1=xm[m % 2][:, g * 512:(g + 1) * 512],
                    op0=ALU.mult, op1=ALU.add), reads=[PS[bank], self.Rmod, Rxm[m % 2]], writes=[Rxm[m % 2]])
            S.dma(S.sp, self.xres_d[b, :, m, :], xm[m % 2], reads=[Rxm[m % 2]], writes=[self.Rxres[b][m]])

    def ffn(self, b, l):
        nc, S = self.nc, self.S
        ps, PS = self.ps, self.PS
        with ExitStack() as fst:
            h2 = self.sb(fst, "h2T", [128, 8, SEQ], BF16)
            Rh2 = S.res("h2T")
            with ExitStack() as st:
                self.normmod(b, l, h2, Rh2, PV["norm_mlp"], 24, 32, st)
                S.barrier()
            self.dump("h2T", h2, [], [128, 8, SEQ], BF16)
            actT = self.sb(fst, "actT", [128, 22, SEQ], BF16)
            Ract = S.res("actT")
            wd0 = self.sb(fst, "wd0", [128, 22, 512], BF16)
            Rwd0 = S.res("wd0")
            S.dma(S.pool, wd0, self.wdn_d[l, :, :, 0:512], writes=[Rwd0])
            with ExitStack() as st:
                wu = [self.sb(st, "wu%d" % i, [128, 8, 512], BF16) for i in range(2)]
                Rwu = [S.res("wu%d" % i) for i in range(2)]
                acc = [self.sb(st, "facc%d" % i, [128, SEQ], F32) for i in range(2)]
                Racc = [S.res("facc%d" % i) for i in range(2)]
                gsl = self.sb(st, "gsl", [128, SEQ], BF16)
                Rgsl = S.res("gsl")
                fw, fb = PV["ffw"], PV["ffb"]

                def ldw(blk):
                    S.dma(S.pool, wu[blk % 2], self.wup_d[l, :, :, blk * 512:(blk + 1) * 512], writes=[Rwu[blk % 2]])

                ldw(0)
                for j in range(22):
                    blk = j // 2
                    if j % 2 == 0 and blk + 1 < 11:
                        ldw(blk + 1)
                    for q in range(2):
                        t = j if q == 0 else 22 + j
                        mo = (j % 2) * 256 + q * 128
                        for g in range(NG):
                            for kc in range(8):
                                S.op(S.pe, lambda: nc.tensor.matmul(ps[:, q * 4 + g, :], wu[blk % 2][:, kc, mo:mo + 128],
                                                                    h2[:, kc, g * 512:(g + 1) * 512], start=(kc == 0),
                                                                    stop=(kc == 7)),
                                     reads=[Rwu[blk % 2], Rh2], writes=[PS[q * 4 + g]], inc=(kc == 7))
                        pq = PS[q * 4:q * 4 + 4]
                        pf = self.psf(q * 4, 4)
                        a = acc[q]
                        S.op(S.act, lambda: nc.scalar.activation(out=a, in_=pf, func=AF.Identity,
                                                                 scale=self.pvec[:, l, fw + t * 3 + 2:fw + t * 3 + 3],
                                                                 bias=self.pvec[:, l, fb + t:fb + t + 1]),
                             reads=pq + [self.Rc], writes=[Racc[q]])
                        for k in range(2):
                            sh = 2 - k
                            S.op(S.dve, lambda: nc.vector.scalar_tensor_tensor(
                                out=a[:, sh:SEQ], in0=pf[:, 0:SEQ - sh],
                                scalar=self.pvec[:, l, fw + t * 3 + k:fw + t * 3 + k + 1], in1=a[:, sh:SEQ],
                                op0=ALU.mult, op1=ALU.add), reads=pq + [self.Rc, Racc[q]], writes=[Racc[q]])
                        if q == 0:
                            S.op(S.act, lambda: nc.scalar.activation(out=gsl, in_=a, func=AF.Silu), reads=[Racc[0]],
                                 writes=[Rgsl])
                    S.op(S.pool, lambda: nc.gpsimd.tensor_tensor(out=actT[:, j, :], in0=gsl, in1=acc[1], op=ALU.mult),
                         reads=[Rgsl, Racc[1]], writes=[Ract])
                S.barrier()
            with ExitStack() as st:
                wd = [wd0, self.sb(st, "wd1", [128, 22, 512], BF16)]
                Rwd = [Rwd0, S.res("wd1")]
                S.dma(S.pool, wd[1], self.wdn_d[l, :, :, 512:1024], writes=[Rwd[1]])
                self.resid_proj(b, l, lambda kc: actT[:, kc, :], [Ract], 22, wd, Rwd, 40, st)
                S.barrier()

    def final(self, b):
        nc, S = self.nc, self.S
        ps, PS = self.ps, self.PS
        with ExitStack() as st:
            xg = [self.sb(st, "fn_xg%d" % i, [128, 8, 512], F32) for i in range(3)]
            Rxg = [S.res("fn_xg%d" % i) for i in range(3)]
            sq = self.sb(st, "fn_sq", [128, 8, 512], BF16)
            Rsq = S.res("fn_sq")
            rstd = self.sb(st, "fn_rstd", [128, 512], F32)
            Rrstd = S.res("fn_rstd")
            ot = [self.sb(st, "fn_ot%d" % i, [128, D], F32) for i in range(2)]
            Rot = [S.res("fn_ot%d" % i) for i in range(2)]
            Rxk = [[S.res("fn_xk%d_%d" % (i, kc)) for kc in range(8)] for i in range(3)]

            def ld(g):
                S.dma(S.sp, xg[g % 3], self.xres_d[b, :, :, g * 512:(g + 1) * 512], reads=self.Rxres[b],
                      writes=[Rxg[g % 3]] + Rxk[g % 3])

            def stats(g):
                i = g % 3
                S.op(S.act, lambda: nc.scalar.activation(out=sq, in_=xg[i], func=AF.Square), reads=[Rxg[i]],
                     writes=[Rsq])
                for kc in range(8):
                    S.op(S.pe, lambda: nc.tensor.matmul(ps[:, g % 2, :], self.onesb, sq[:, kc, :], start=(kc == 0),
                                                        stop=(kc == 7)), reads=[Rsq, self.Rc], writes=[PS[g % 2]],
                         inc=(kc == 7))

            ld(0)
            ld(1)
            ld(2)
            stats(0)
            oi = 0
            for g in range(NG):
                i = g % 3
                S.op(S.act, lambda: nc.scalar.activation(out=rstd, in_=ps[:, g % 2, :], func=AF.Ln, scale=1.0 / D,
                                                         bias=self.epsb), reads=[PS[g % 2], self.Rc], writes=[Rrstd])
                S.op(S.act, lambda: nc.scalar.activation(out=rstd, in_=rstd, func=AF.Exp, scale=-0.5), reads=[Rrstd],
                     writes=[Rrstd])
                S.op(S.dve, lambda: nc.vector.tensor_tensor(out=xg[i], in0=xg[i],
                                                             in1=rstd.unsqueeze(1).to_broadcast([128, 8, 512]),
                                                             op=ALU.mult), reads=[Rrstd, Rxg[i]], writes=[Rxg[i]])
                if g + 1 < NG:
                    stats(g + 1)
                for kc in range(8):
                    if kc % 2 == 0:
                        S.op(S.act, lambda: nc.scalar.activation(out=xg[i][:, kc, :], in_=xg[i][:, kc, :], func=AF.Copy,
                                                                 scale=self.fnorm[:, kc:kc + 1]),
                             reads=[Rxg[i], self.Rc], writes=[Rxk[i][kc]])
                    else:
                        S.op(S.dve, lambda: nc.vector.tensor_scalar(out=xg[i][:, kc, :], in0=xg[i][:, kc, :],
                                                                     scalar1=self.fnorm[:, kc:kc + 1], scalar2=None,
                                                                     op0=ALU.mult),
                             reads=[Rxg[i], self.Rc], writes=[Rxk[i][kc]])
                for t in range(4):
                    o = oi % 2
                    oi += 1
                    for half in range(2):
                        for kq in range(4):
                            kc = half * 4 + kq
                            bank = 4 + (t % 2) * 2 + half
                            S.op(S.pe, lambda: nc.tensor.transpose(ps[:, bank, kq * 128:(kq + 1) * 128],
                                                                   xg[i][:, kc, t * 128:(t + 1) * 128], self.identf),
                                 reads=[Rxk[i][kc], self.Rc], writes=[PS[bank]], inc=(kq == 3))
                    b0 = 4 + (t % 2) * 2
                    S.op(S.act if t % 2 == 0 else S.dve,
                         (lambda: nc.scalar.copy(ot[o], self.psf(b0, 2))) if t % 2 == 0 else
                         (lambda: nc.vector.tensor_copy(ot[o], self.psf(b0, 2))),
                         reads=PS[b0:b0 + 2], writes=[Rot[o]])
                    r0 = (g * 4 + t) * 128
                    S.dma(S.sp, self.out_d[b, r0:r0 + 128, :], ot[o], reads=[Rot[o]], writes=[self.Rout])
                if g + 3 < NG:
                    ld(g + 3)


def _pk(v):
    v = np.asarray(v, np.float32)
    return np.ascontiguousarray(v.reshape(-1, 128).T)


def _wl(w):
    k, n = w.shape
    return np.ascontiguousarray(w.reshape(k // 128, 128, n).transpose(1, 0, 2))


def _prep_shared(inp):
    L = DEPTH
    f = lambda k: np.asarray(inp[k], np.float32)
    pvec = np.zeros((128, L, NPV), np.float32)
    hvec = np.zeros((128, L, 48), np.float32)
    w_in, w_uq, w_ukv, w_up = [], [], [], []
    for l in range(L):
        pvec[:, l, PV["norm_mix"]:PV["norm_mix"] + 8] = _pk(f("norm_mix")[l])
        cw = f("conv_w")[l]
        pvec[:, l, PV["conv_w"]:PV["conv_w"] + 48] = cw.reshape(4, 12, 128).transpose(2, 1, 0).reshape(128, 48)
        pvec[:, l, PV["conv_b"]:PV["conv_b"] + 12] = _pk(f("conv_b")[l])
        pvec[:, l, PV["q_norm"]:PV["q_norm"] + 3] = _pk(f("q_norm")[l])
        pvec[:, l, PV["kv_norm"]:PV["kv_norm"] + 2] = _pk(f("kv_norm")[l])
        pvec[:, l, PV["attn_norm"]:PV["attn_norm"] + 8] = _pk(f("attn_norm")[l])
        pvec[:, l, PV["ssd_norm"]:PV["ssd_norm"] + 8] = _pk(f("ssd_norm")[l])
        pvec[:, l, PV["norm_mlp"]:PV["norm_mlp"] + 8] = _pk(f("norm_mlp")[l])
        fw = f("conv_ff_w")[l]
        pvec[:, l, PV["ffw"]:PV["ffw"] + 132] = fw.reshape(3, 44, 128).transpose(2, 1, 0).reshape(128, 132)
        pvec[:, l, PV["ffb"]:PV["ffb"] + 44] = _pk(f("conv_ff_b")[l])
        pvec[:, l, PV["b_ada"]:PV["b_ada"] + 48] = _pk(f("b_ada")[l])
        hvec[:, l, 0:16] = f("dt_bias")[l][None, :]
        hvec[:, l, 16:32] = f("a_log")[l][None, :]
        hvec[:, l, 32:48] = f("d_skip")[l][None, :]
        wi = f("w_in")[l]
        kr = wi[:, 3216:3280]
        ext = np.concatenate([wi[:, 1024:2560], wi[:, 2576:2960], wi[:, 2960:3216], kr,
                              kr[:, 32:64], kr[:, 0:32], wi[:, 0:1024], wi[:, 2560:2576]], axis=1)
        w_in.append(_wl(ext))
        wq = f("w_uq")[l].reshape(384, 8, 192)
        wq_ext = np.concatenate([wq[:, :, 0:128], wq[:, :, 128:192], wq[:, :, 160:192], wq[:, :, 128:160]], axis=2)
        w_uq.append(_wl(wq_ext.reshape(384, 2048)))
        wk = f("w_ukv")[l].reshape(256, 8, 256)
        w_ukv.append(_wl(np.concatenate([wk[:, :, 0:128].reshape(256, 1024), wk[:, :, 128:256].reshape(256, 1024)],
                                        axis=1)))
        wu = f("w_up")[l]
        wu_p = np.stack([wu[:, 0:D_FF].reshape(D, 22, 128), wu[:, D_FF:].reshape(D, 22, 128)], axis=2)
        w_up.append(_wl(wu_p.reshape(D, 2 * D_FF)))
    consts = np.zeros((128, 4, 128), np.float32)
    i = np.arange(128)
    consts[:, 0, :] = np.eye(128)
    consts[:, 1, :] = (i[:, None] <= i[None, :])
    consts[:, 2, :] = (i[:, None] > i[None, :])
    consts[:, 3, :] = 1.0
    inv_freq = (1.0 / (10000.0 ** (np.arange(0, 64, 2, dtype=np.float32) / 64))).astype(np.float32)
    ropec = np.zeros((128, 2), np.float32)
    ropec[:, 0] = np.tile(inv_freq, 4)
    ropec[:, 1] = np.tile(np.concatenate([-np.ones(32), np.ones(32)]), 2)
    return {
        "w_ada": np.stack([_wl(f("w_ada")[l]) for l in range(L)]),
        "pvec": pvec, "hvec": hvec, "fnorm": _pk(f("final_norm")),
        "w_in": np.stack(w_in), "w_uq": np.stack(w_uq), "w_ukv": np.stack(w_ukv),
        "w_out": np.stack([_wl(f("w_out")[l]) for l in range(L)]),
        "w_up": np.stack(w_up),
        "w_down": np.stack([_wl(f("w_down")[l]) for l in range(L)]),
        "consts": consts, "ropec": ropec,
    }


def _core_inputs(inp, shared, core, nb=NBC):
    b0 = core * NBC
    c = np.asarray(inp["c"], np.float32)[b0:b0 + NBC]
    cT = np.ascontiguousarray(c.T.reshape(8, 128, NBC).transpose(1, 0, 2))
    m = dict(shared)
    m["x"] = np.ascontiguousarray(np.asarray(inp["x"], np.float32)[b0:b0 + nb])
    m["cT"] = cT
    m["pos"] = np.ascontiguousarray(np.asarray(inp["positions"], np.int32)[b0:b0 + nb])
    return m


_NC_CACHE = {}


def kernel(**inputs):
    shared = _prep_shared(inputs)
    if "nc" not in _NC_CACHE:
        _NC_CACHE["nc"] = Builder().nc
    nc = _NC_CACHE["nc"]
    in_maps = [_core_inputs(inputs, shared, core) for core in range(NCORES)]
    res = run_bass_kernel_spmd(nc, in_maps, core_ids=list(range(NCORES)))
    out = np.concatenate([np.asarray(r["out"], np.float32) for r in res.results], axis=0)
    return out
```

```python
import numpy as np
from contextlib import ExitStack
import concourse.bass as bass
import concourse.mybir as mybir
from concourse.bass_utils import run_bass_kernel_spmd

F32 = mybir.dt.float32
BF16 = mybir.dt.bfloat16
I32 = mybir.dt.int32
AF = mybir.ActivationFunctionType
ALU = mybir.AluOpType

D = 1024
SEQ = 2048
NCORES = 8
NBC = 4
DEPTH = 2
D_FF = 2816
EPS = 1e-6
NG = SEQ // 512
NCH = SEQ // 128
WIN_COLS = 3344
Z0 = 2304
SM_SCALE = 192.0 ** -0.5

PV = {}
_o = 0
for _n, _c in [("norm_mix", 8), ("conv_w", 48), ("conv_b", 12), ("q_norm", 3), ("kv_norm", 2),
               ("attn_norm", 8), ("ssd_norm", 8), ("norm_mlp", 8), ("ffw", 132), ("ffb", 44), ("b_ada", 48)]:
    PV[_n] = _o
    _o += _c
NPV = _o


class Res:
    __slots__ = ("name", "w", "r", "excl", "ds", "ep")

    def __init__(self, name, excl=False):
        self.name = name
        self.w = None
        self.r = []
        self.excl = excl
        self.ds = None
        self.ep = -1


class Eng:
    def __init__(self, nc, name, h):
        self.name = name
        self.h = h
        self.sem = nc.alloc_semaphore("sem_" + name)
        self.cnt = 0
        self.waited = {}
        self.pend = []


class Sched:
    def __init__(self, nc):
        self.nc = nc
        self.pe = Eng(nc, "pe", nc.tensor)
        self.act = Eng(nc, "act", nc.scalar)
        self.dve = Eng(nc, "dve", nc.vector)
        self.pool = Eng(nc, "pool", nc.gpsimd)
        self.sp = Eng(nc, "sp", nc.sync)
        self.engs = [self.pe, self.act, self.dve, self.pool, self.sp]
        self.nsem = 0
        self.dma_toks = []
        self.all_dsems = []
        self.free_dsems = []
        self.epoch = 0

    def res(self, name, excl=False):
        return Res(name, excl)

    def _wait(self, e, tok):
        sem, val, owner = tok
        key = id(sem)
        if e.waited.get(key, 0) >= val:
            return
        e.h.wait_ge(sem, val)
        e.waited[key] = val

    def _deps(self, e, reads, writes):
        toks = []
        for r in reads:
            if r.excl:
                if r.w is not None:
                    toks.append(r.w)
                toks.extend(r.r)
            elif r.w is not None:
                toks.append(r.w)
        for w in writes:
            if w.w is not None:
                toks.append(w.w)
            toks.extend(w.r)
        for t in toks:
            if t[2] is e and e is self.pe:
                continue
            self._wait(e, t)

    def _commit(self, tok, reads, writes):
        for r in reads:
            if r.excl:
                r.w = tok
                r.r = []
            else:
                r.r.append(tok)
                if len(r.r) > 48:
                    r.r = r.r[-48:]
        for w in writes:
            w.w = tok
            w.r = []

    def op(self, e, fn, reads=(), writes=(), inc=True):
        for o in self.engs:
            assert o is e or not o.pend, "pending non-inc ops on another engine"
        self._deps(e, reads, writes)
        ins = fn()
        if not inc:
            e.pend.append((reads, writes))
            return None
        e.cnt += 1
        ins.then_inc(e.sem, 1)
        tok = (e.sem, e.cnt, e)
        for (r, w) in e.pend:
            self._commit(tok, r, w)
        e.pend = []
        self._commit(tok, reads, writes)
        return tok

    def dma(self, q, out, in_, reads=(), writes=(), key=None):
        for o in self.engs:
            assert not o.pend
        self._deps(q, reads, writes)
        kr = key or (writes[0] if writes else reads[0])
        if kr.ds is None or kr.ep != self.epoch:
            if self.free_dsems:
                kr.ds = self.free_dsems.pop()
            else:
                kr.ds = [self.nc.alloc_semaphore("dsem%d" % self.nsem), 0]
                self.nsem += 1
                self.all_dsems.append(kr.ds)
            kr.ep = self.epoch
        ins = q.h.dma_start(out=out, in_=in_)
        kr.ds[1] += 16
        ins.then_inc(kr.ds[0], 16)
        tok = (kr.ds[0], kr.ds[1], None)
        self._commit(tok, reads, writes)
        self.dma_toks.append(tok)
        return tok

    def barrier(self):
        sp = self.sp
        last = {}
        for t in self.dma_toks:
            k = id(t[0])
            if k not in last or last[k][1] < t[1]:
                last[k] = t
        for t in last.values():
            self._wait(sp, t)
        self.dma_toks = []
        for o in self.engs:
            if o is not sp and o.cnt:
                self._wait(sp, (o.sem, o.cnt, o))
        sp.cnt += 1
        sp.h.sem_inc(sp.sem, 1)
        tok = (sp.sem, sp.cnt, sp)
        for e in self.engs:
            if e is not sp:
                self._wait(e, tok)
        self.epoch += 1
        self.free_dsems = list(self.all_dsems)


class Builder:
    def __init__(self, nb=NBC, depth=DEPTH, dbg=None):
        self.nb = nb
        self.depth = depth
        self.dbg = dbg or []
        self.dbg_out = {}
        nc = bass.Bass("TRN2", target_bir_lowering=False)
        self.nc = nc
        self.S = Sched(nc)
        S = self.S

        def din(name, shape, dt=F32):
            return nc.dram_tensor(name, list(shape), dt, kind="ExternalInput").ap()

        self.x_d = din("x", [nb, SEQ, D])
        self.cT_d = din("cT", [128, 8, NBC])
        self.pos_d = din("pos", [nb, SEQ], I32)
        self.wada_d = din("w_ada", [DEPTH, 128, 8, 6 * D])
        self.pvec_d = din("pvec", [128, DEPTH, NPV])
        self.hvec_d = din("hvec", [128, DEPTH, 48])
        self.fnorm_d = din("fnorm", [128, 8])
        self.win_d = din("w_in", [DEPTH, 128, 8, WIN_COLS])
        self.wuq_d = din("w_uq", [DEPTH, 128, 3, 2048])
        self.wukv_d = din("w_ukv", [DEPTH, 128, 2, 2048])
        self.wout_d = din("w_out", [DEPTH, 128, 16, D])
        self.wup_d = din("w_up", [DEPTH, 128, 8, 2 * D_FF])
        self.wdn_d = din("w_down", [DEPTH, 128, 22, D])
        self.cst_d = din("consts", [128, 4, 128])
        self.rope_d = din("ropec", [128, 2])
        self.out_d = nc.dram_tensor("out", [nb, SEQ, D], F32, kind="ExternalOutput").ap()
        self.xres_d = nc.dram_tensor("xres", [nb, 128, 8, SEQ], F32, kind="Internal").ap()
        self.Rxres = [[S.res("xres%d_%d" % (b, k)) for k in range(8)] for b in range(nb)]
        self.Rout = S.res("out")

        self.ps = nc.alloc_psum_tensor("ps", [128, 8, 512], F32).ap()
        self.PS = [S.res("ps%d" % i, excl=True) for i in range(8)]
        self.build()

    def sb(self, stack, name, shape, dt):
        self._uid = getattr(self, "_uid", 0) + 1
        h = stack.enter_context(self.nc.sbuf_tensor("%s_u%d" % (name, self._uid), list(shape), dt))
        return h.ap()

    def psf(self, b0, nb_):
        return self.ps[:, b0:b0 + nb_, :].rearrange("p a b -> p (a b)")

    def psb(self, b):
        return self.ps[:, b, :].bitcast(BF16)

    def dump(self, name, ap, res_list, shape, dt=F32):
        if name not in self.dbg:
            return
        S = self.S
        o = self.nc.dram_tensor("dbg_" + name, list(shape), dt, kind="ExternalOutput").ap()
        R = S.res("dbg_" + name)
        S.dma(S.sp, o, ap, reads=res_list, writes=[R])
        self.dbg_out[name] = R

    def build(self):
        nc, S = self.nc, self.S
        with ExitStack() as top:
            self.cst = self.sb(top, "cst", [128, 4, 128], F32)
            self.cstb = self.sb(top, "cstb", [128, 4, 128], BF16)
            self.ropec = self.sb(top, "ropec", [128, 2], F32)
            self.pvec = self.sb(top, "pvec", [128, DEPTH, NPV], F32)
            self.hvec = self.sb(top, "hvec", [128, DEPTH, 48], F32)
            self.fnorm = self.sb(top, "fnorm", [128, 8], F32)
            self.modT = self.sb(top, "modT", [128, DEPTH, 48, NBC], F32)
            self.ahead = self.sb(top, "ahead", [128, DEPTH, 16], F32)
            self.epsb = self.sb(top, "epsb", [128, 1], F32)
            S.op(S.dve, lambda: nc.vector.memset(self.epsb, EPS), writes=[S.res("eps")])
            self.Rc = S.res("consts")
            self.Rmod = S.res("modT")
            for dst, src in [(self.cst, self.cst_d), (self.ropec, self.rope_d), (self.pvec, self.pvec_d),
                             (self.hvec, self.hvec_d), (self.fnorm, self.fnorm_d)]:
                S.dma(S.sp, dst, src, writes=[self.Rc])
            S.op(S.dve, lambda: nc.vector.tensor_copy(self.cstb, self.cst), reads=[self.Rc], writes=[self.Rc])
            S.op(S.act, lambda: nc.scalar.activation(out=self.ahead, in_=self.hvec[:, :, 16:32], func=AF.Exp),
                 reads=[self.Rc], writes=[self.Rc])
            S.op(S.dve, lambda: nc.vector.tensor_scalar(out=self.ahead, in0=self.ahead, scalar1=-1.0, scalar2=None,
                                                         op0=ALU.mult), reads=[self.Rc], writes=[self.Rc])
            self.identf = self.cst[:, 0, :]
            self.Lf = self.cst[:, 1, :]
            self.Uf = self.cst[:, 2, :]
            self.onesf = self.cst[:, 3, :]
            self.identb = self.cstb[:, 0, :]
            self.Lb = self.cstb[:, 1, :]
            self.Ub = self.cstb[:, 2, :]
            self.negb = self.sb(top, "negb", [128, 128], BF16)
            S.op(S.dve, lambda: nc.vector.tensor_scalar(out=self.negb, in0=self.cst[:, 2, :], scalar1=-30000.0,
                                                         scalar2=None, op0=ALU.mult), reads=[self.Rc], writes=[self.Rc])
            self.onesb = self.cstb[:, 3, :]

            self.prologue_mod()
            S.barrier()
            for b in range(self.nb):
                with ExitStack() as bst:
                    self.cosT = self.sb(bst, "cosT", [128, SEQ], F32)
                    self.sinT = self.sb(bst, "sinT", [128, SEQ], F32)
                    self.Rrope = S.res("rope")
                    self.rope_tables(b)
                    self.load_x(b)
                    S.barrier()
                    for l in range(self.depth):
                        self.mixer(b, l)
                        self.ffn(b, l)
                    self.final(b)
                    S.barrier()
            S.barrier()

    def prologue_mod(self):
        nc, S = self.nc, self.S
        with ExitStack() as st:
            cT = self.sb(st, "cT", [128, 8, NBC], F32)
            cact = self.sb(st, "cact", [128, 8, NBC], BF16)
            wb = [self.sb(st, "wada%d" % i, [128, 8, 512], BF16) for i in range(2)]
            Rw = [S.res("wada%d" % i) for i in range(2)]
            Rct = S.res("cT")
            S.dma(S.sp, cT, self.cT_d, writes=[Rct])
            S.op(S.act, lambda: nc.scalar.activation(out=cact, in_=cT, func=AF.Silu), reads=[Rct], writes=[Rct])
            for l in range(self.depth):
                bank = l % 2
                for blk in range(12):
                    i = blk % 2
                    S.dma(S.pool, wb[i], self.wada_d[l, :, :, blk * 512:(blk + 1) * 512], writes=[Rw[i]])
                    for mt in range(4):
                        T = blk * 4 + mt
                        for kc in range(8):
                            S.op(S.pe, lambda: nc.tensor.matmul(self.ps[:, bank, T * NBC:(T + 1) * NBC],
                                                                wb[i][:, kc, mt * 128:(mt + 1) * 128], cact[:, kc, :],
                                                                start=(kc == 0), stop=(kc == 7)),
                                 reads=[Rw[i], Rct], writes=[self.PS[bank]], inc=(kc == 7))
                bada = self.pvec[:, l, PV["b_ada"]:PV["b_ada"] + 48]
                S.op(S.dve, lambda: nc.vector.tensor_tensor(
                    out=self.modT[:, l, :, :], in0=self.ps[:, bank, 0:48 * NBC].rearrange("p (t b) -> p t b", b=NBC),
                    in1=bada.unsqueeze(2).to_broadcast([128, 48, NBC]), op=ALU.add),
                    reads=[self.PS[bank], self.Rc], writes=[self.Rmod])
                for t0 in (8, 32):
                    S.op(S.dve, lambda: nc.vector.tensor_scalar(
                        out=self.modT[:, l, t0:t0 + 8, :], in0=self.modT[:, l, t0:t0 + 8, :], scalar1=1.0, scalar2=None,
                        op0=ALU.add), reads=[self.Rmod], writes=[self.Rmod])
            S.barrier()

    def rope_tables(self, b):
        nc, S = self.nc, self.S
        with ExitStack() as st:
            posi = self.sb(st, "posi", [128, SEQ], I32)
            ang = self.sb(st, "ang", [128, SEQ], F32)
            t1 = self.sb(st, "rt1", [128, SEQ], F32)
            t2 = self.sb(st, "rt2", [128, SEQ], F32)
            ki = self.sb(st, "rki", [128, SEQ], I32)
            R = S.res("ropetmp")
            S.dma(S.sp, posi, self.pos_d[b:b + 1, :].partition_broadcast(128), writes=[R])
            S.op(S.dve, lambda: nc.vector.tensor_copy(ang, posi), reads=[R], writes=[R])
            S.op(S.dve, lambda: nc.vector.tensor_scalar(out=ang, in0=ang, scalar1=self.ropec[:, 0:1], scalar2=None,
                                                         op0=ALU.mult), reads=[R, self.Rc], writes=[R])
            C1 = 6.28125
            C2 = float(2 * np.pi - 6.28125)
            for which, dst in ((0, self.sinT), (1, self.cosT)):
                if which == 1:
                    S.op(S.dve, lambda: nc.vector.tensor_scalar(out=t1, in0=ang, scalar1=float(np.pi / 2), scalar2=None,
                                                                 op0=ALU.add), reads=[R], writes=[R])
                    src = t1
                else:
                    S.op(S.dve, lambda: nc.vector.tensor_copy(t1, ang), reads=[R], writes=[R])
                    src = t1
                S.op(S.dve, lambda: nc.vector.tensor_scalar(out=ki, in0=src, scalar1=float(1.0 / (2 * np.pi)),
                                                             scalar2=None, op0=ALU.mult), reads=[R], writes=[R])
                S.op(S.dve, lambda: nc.vector.tensor_copy(t2, ki), reads=[R], writes=[R])
                S.op(S.dve, lambda: nc.vector.scalar_tensor_tensor(out=t1, in0=t2, scalar=-C1, in1=t1, op0=ALU.mult,
                                                                    op1=ALU.add), reads=[R], writes=[R])
                S.op(S.dve, lambda: nc.vector.scalar_tensor_tensor(out=t1, in0=t2, scalar=-C2, in1=t1, op0=ALU.mult,
                                                                    op1=ALU.add), reads=[R], writes=[R])
                S.op(S.dve, lambda: nc.vector.tensor_scalar(out=t2, in0=t1, scalar1=float(np.pi),
                                                             scalar2=float(-2 * np.pi), op0=ALU.is_gt, op1=ALU.mult),
                     reads=[R], writes=[R])
                S.op(S.dve, lambda: nc.vector.tensor_tensor(out=t1, in0=t1, in1=t2, op=ALU.add), reads=[R], writes=[R])
                S.op(S.dve, lambda: nc.vector.tensor_scalar(out=t1, in0=t1, scalar1=float(-np.pi), scalar2=float(np.pi),
                                                             op0=ALU.max, op1=ALU.min), reads=[R], writes=[R])
                if which == 0:
                    S.op(S.act, lambda: nc.scalar.activation(out=t2, in_=t1, func=AF.Sin), reads=[R], writes=[R])
                    S.op(S.dve, lambda: nc.vector.tensor_scalar(out=dst, in0=t2, scalar1=self.ropec[:, 1:2],
                                                                 scalar2=None, op0=ALU.mult),
                         reads=[R, self.Rc], writes=[self.Rrope])
                else:
                    S.op(S.act, lambda: nc.scalar.activation(out=dst, in_=t1, func=AF.Sin), reads=[R],
                         writes=[self.Rrope])
            S.barrier()

    def load_x(self, b):
        nc, S = self.nc, self.S
        with ExitStack() as st:
            xt = [[self.sb(st, "xtok%d_%d" % (i, t), [128, D], F32) for t in range(4)] for i in range(2)]
            Rxt = [[S.res("xtok%d_%d" % (i, t)) for t in range(4)] for i in range(2)]
            xg = [self.sb(st, "xTg%d" % i, [128, 8, 512], F32) for i in range(2)]
            Rxg = [[S.res("xTg%d_%d" % (i, hf)) for hf in range(2)] for i in range(2)]
            def ldx(g):
                for t in range(4):
                    r0 = (g * 4 + t) * 128
                    S.dma(S.sp, xt[g % 2][t], self.x_d[b, r0:r0 + 128, :], writes=[Rxt[g % 2][t]])

            ldx(0)
            for g in range(NG):
                i = g % 2
                if g + 1 < NG:
                    ldx(g + 1)
                for half in range(2):
                    for kq in range(4):
                        kc = half * 4 + kq
                        for t in range(4):
                            S.op(S.pe, lambda: nc.tensor.transpose(self.ps[:, kc, t * 128:(t + 1) * 128],
                                                                   xt[i][t][:, kc * 128:(kc + 1) * 128], self.identf),
                                 reads=[Rxt[i][t], self.Rc], writes=[self.PS[kc]], inc=(t == 3))
                    eng = S.act if half == 0 else S.dve
                    dst = xg[i][:, half * 4:(half + 1) * 4, :]
                    src = self.ps[:, half * 4:(half + 1) * 4, :]
                    if half == 0:
                        S.op(S.act, lambda: nc.scalar.copy(dst, src), reads=self.PS[0:4], writes=[Rxg[i][0]])
                    else:
                        S.op(S.dve, lambda: nc.vector.tensor_copy(dst, src), reads=self.PS[4:8], writes=[Rxg[i][1]])
                S.dma(S.sp, self.xres_d[b, :, :, g * 512:(g + 1) * 512], xg[i], reads=Rxg[i], writes=self.Rxres[b])
            S.barrier()

    def normmod(self, b, l, hT, RhT, normcol, sh0, sc0, st):
        nc, S = self.nc, self.S
        NB_ = 3
        xg = [self.sb(st, "nm_xg%d" % i, [128, 8, 512], F32) for i in range(NB_)]
        Rxg = [S.res("nm_xg%d" % i) for i in range(NB_)]
        sq = self.sb(st, "nm_sq", [128, 8, 512], BF16)
        Rsq = S.res("nm_sq")
        rstd = self.sb(st, "nm_rstd", [128, 512], F32)
        Rrstd = S.res("nm_rstd")
        A = self.sb(st, "nm_A", [128, 8], F32)
        RA = S.res("nm_A")
        RhTk = [S.res("nm_hTk%d" % kc) for kc in range(8)]
        S.op(S.dve, lambda: nc.vector.tensor_tensor(out=A, in0=self.pvec[:, l, normcol:normcol + 8],
                                                     in1=self.modT[:, l, sc0:sc0 + 8, b], op=ALU.mult),
             reads=[self.Rc, self.Rmod], writes=[RA])
        def ld(g):
            S.dma(S.sp, xg[g % NB_], self.xres_d[b, :, :, g * 512:(g + 1) * 512], reads=self.Rxres[b],
                  writes=[Rxg[g % NB_]])

        def stats(g):
            i = g % NB_
            S.op(S.act, lambda: nc.scalar.activation(out=sq, in_=xg[i], func=AF.Square), reads=[Rxg[i]], writes=[Rsq])
            for kc in range(8):
                S.op(S.pe, lambda: nc.tensor.matmul(self.ps[:, g % 2, :], self.onesb, sq[:, kc, :], start=(kc == 0),
                                                    stop=(kc == 7)), reads=[Rsq, self.Rc], writes=[self.PS[g % 2]],
                     inc=(kc == 7))

        ld(0)
        ld(1)
        ld(2)
        stats(0)
        for g in range(NG):
            i = g % NB_
            bank = g % 2
            S.op(S.act, lambda: nc.scalar.activation(out=rstd, in_=self.ps[:, bank, :], func=AF.Ln, scale=1.0 / D,
                                                     bias=self.epsb), reads=[self.PS[bank], self.Rc], writes=[Rrstd])
            S.op(S.act, lambda: nc.scalar.activation(out=rstd, in_=rstd, func=AF.Exp, scale=-0.5), reads=[Rrstd],
                 writes=[Rrstd])
            S.op(S.dve, lambda: nc.vector.tensor_tensor(out=xg[i], in0=xg[i],
                                                         in1=rstd.unsqueeze(1).to_broadcast([128, 8, 512]), op=ALU.mult),
                 reads=[Rrstd, Rxg[i]], writes=[Rxg[i]])
            if g + 1 < NG:
                stats(g + 1)
            for kc in range(8):
                o_ = hT[:, kc, g * 512:(g + 1) * 512]
                if kc % 2 == 0:
                    S.op(S.act, lambda: nc.scalar.activation(out=o_, in_=xg[i][:, kc, :], func=AF.Identity,
                                                             scale=A[:, kc:kc + 1],
                                                             bias=self.modT[:, l, sh0 + kc, b:b + 1]),
                         reads=[Rxg[i], RA, self.Rmod], writes=[RhTk[kc]])
                else:
                    S.op(S.dve, lambda: nc.vector.tensor_scalar(out=o_, in0=xg[i][:, kc, :], scalar1=A[:, kc:kc + 1],
                                                                 scalar2=self.modT[:, l, sh0 + kc, b:b + 1],
                                                                 op0=ALU.mult, op1=ALU.add),
                         reads=[Rxg[i], RA, self.Rmod], writes=[RhTk[kc]])
            if g + 3 < NG:
                ld(g + 3)

    def mixer(self, b, l):
        nc, S = self.nc, self.S
        with ExitStack() as mst:
            hT = self.sb(mst, "hT", [128, 8, SEQ], BF16)
            RhT = [S.res("hT%d" % c) for c in range(NCH)]
            cqn = self.sb(mst, "cqn", [128, 3, SEQ], BF16)
            ckvn = self.sb(mst, "ckvn", [128, 2, SEQ], BF16)
            krT = self.sb(mst, "krT", [128, SEQ], BF16)
            Rlat = S.res("lat")
            S.op(S.pool, lambda: nc.gpsimd.memset(krT[64:128, :], 0.0), writes=[Rlat])
            with ExitStack() as st:
                self.normmod(b, l, hT, S.res("hTall"), PV["norm_mix"], 0, 8, st)
                S.barrier()
            self.dump("hT", hT, [], [128, 8, SEQ], BF16)
            with ExitStack() as st:
                xbcT = self.sb(st, "xbcT", [128, 12, SEQ], BF16)
                Rxbc = S.res("xbcT")
                wz = self.sb(st, "wz", [128, 8, 1040], BF16)
                Rwz = S.res("wz")
                S.dma(S.pool, wz, self.win_d[l, :, :, Z0:Z0 + 1040], writes=[Rwz])
                with ExitStack() as st2:
                    self.in_proj(b, l, hT, xbcT, Rxbc, cqn, ckvn, krT, Rlat, st2)
                    S.barrier()
                self.dump("xbcT", xbcT, [], [128, 12, SEQ], BF16)
                self.dump("cqn", cqn, [], [128, 3, SEQ], BF16)
                self.dump("ckvn", ckvn, [], [128, 2, SEQ], BF16)
                self.dump("krT", krT[0:64, :], [], [64, SEQ], BF16)
                with ExitStack() as st2:
                    self.ssd(b, l, hT, RhT, xbcT, Rxbc, wz, Rwz, st2)
                    S.barrier()
            self.dump("yssdT", hT, [], [128, 8, SEQ], BF16)
            with ExitStack() as st:
                oT = self.sb(st, "oT", [128, 8, SEQ], BF16)
                RoT = S.res("oT")
                with ExitStack() as st2:
                    self.attention(b, l, cqn, ckvn, krT, Rlat, oT, RoT, st2)
                    S.barrier()
                self.dump("oT", oT, [], [128, 8, SEQ], BF16)
                with ExitStack() as st2:
                    self.out_proj(b, l, hT, oT, RoT, st2)
                    S.barrier()

    def in_proj(self, b, l, hT, xbcT, Rxbc, cqn, ckvn, krT, Rlat, st):
        nc, S = self.nc, self.S
        wblk = [self.sb(st, "wblk%d" % i, [128, 8, 512], BF16) for i in range(2)]
        Rw = [S.res("wblk%d" % i) for i in range(2)]
        acc = [self.sb(st, "acc%d" % i, [128, SEQ], F32) for i in range(2)]
        Racc = [S.res("acc%d" % i) for i in range(2)]
        RhT = S.res("hT_ro")
        cw = PV["conv_w"]
        nblk = 5
        qi = 0

        def load_blk(blk):
            c0 = blk * 512
            n = min(512, Z0 - c0)
            S.dma(S.pool, wblk[blk % 2][:, :, 0:n], self.win_d[l, :, :, c0:c0 + n], writes=[Rw[blk % 2]])

        def proj_tile(m, q):
            blk, mo = m // 4, (m % 4) * 128
            w = wblk[blk % 2]
            for g in range(NG):
                for kc in range(8):
                    S.op(S.pe, lambda: nc.tensor.matmul(self.ps[:, q * 4 + g, :], w[:, kc, mo:mo + 128],
                                                        hT[:, kc, g * 512:(g + 1) * 512], start=(kc == 0),
                                                        stop=(kc == 7)),
                         reads=[Rw[blk % 2], RhT], writes=[self.PS[q * 4 + g]], inc=(kc == 7))

        load_blk(0)
        for m in range(12):
            if m % 4 == 0 and m // 4 + 1 < nblk:
                load_blk(m // 4 + 1)
            q = m % 2
            proj_tile(m, q)
            pq = self.PS[q * 4:q * 4 + 4]
            pf = self.psf(q * 4, 4)
            a = acc[q]
            S.op(S.act, lambda: nc.scalar.activation(out=a, in_=pf, func=AF.Identity,
                                                     scale=self.pvec[:, l, cw + m * 4 + 3:cw + m * 4 + 4],
                                                     bias=self.pvec[:, l, PV["conv_b"] + m:PV["conv_b"] + m + 1]),
                 reads=pq + [self.Rc], writes=[Racc[q]])
            if m >= 1:
                S.op(S.act, lambda: nc.scalar.activation(out=xbcT[:, m - 1, :], in_=acc[1 - q], func=AF.Silu),
                     reads=[Racc[1 - q]], writes=[Rxbc])
            for k in range(3):
                sh = 3 - k
                S.op(S.dve, lambda: nc.vector.scalar_tensor_tensor(
                    out=a[:, sh:SEQ], in0=pf[:, 0:SEQ - sh], scalar=self.pvec[:, l, cw + m * 4 + k:cw + m * 4 + k + 1],
                    in1=a[:, sh:SEQ], op0=ALU.mult, op1=ALU.add), reads=pq + [self.Rc, Racc[q]], writes=[Racc[q]])
        S.op(S.act, lambda: nc.scalar.activation(out=xbcT[:, 11, :], in_=acc[1], func=AF.Silu), reads=[Racc[1]],
             writes=[Rxbc])
        sqh = [self.sb(st, "ip_sqh%d" % i, [128, 1024], BF16) for i in range(2)]
        Rsqh = [S.res("ip_sqh%d" % i) for i in range(2)]
        hq = 0
        pend_ss = None
        for (m0, nt, dst, ncol, dim) in ((12, 3, cqn, PV["q_norm"], 384), (15, 2, ckvn, PV["kv_norm"], 256)):
            for i in range(nt):
                m = m0 + i
                if m == 16:
                    load_blk(4)
                blk, mo = m // 4, (m % 4) * 128
                w = wblk[blk % 2]
                for hf in range(2):
                    hb = (hq % 2) * 2
                    for g2 in range(2):
                        g = hf * 2 + g2
                        for kc in range(8):
                            S.op(S.pe, lambda: nc.tensor.matmul(self.ps[:, hb + g2, :], w[:, kc, mo:mo + 128],
                                                                hT[:, kc, g * 512:(g + 1) * 512], start=(kc == 0),
                                                                stop=(kc == 7)),
                                 reads=[Rw[blk % 2], RhT], writes=[self.PS[hb + g2]], inc=(kc == 7))
                    pq = self.PS[hb:hb + 2]
                    pf = self.psf(hb, 2)
                    ts_ = slice(hf * 1024, (hf + 1) * 1024)
                    S.op(S.dve, lambda: nc.vector.tensor_copy(dst[:, i, ts_], pf), reads=pq, writes=[Rlat])
                    S.op(S.act, lambda: nc.scalar.activation(out=sqh[hq % 2], in_=pf, func=AF.Square), reads=pq,
                         writes=[Rsqh[hq % 2]])
                    if pend_ss is not None:
                        pend_ss()

                    def ss_mm(hf=hf, i=i, nt=nt, k=hq % 2):
                        for g2 in range(2):
                            g = hf * 2 + g2
                            S.op(S.pe, lambda: nc.tensor.matmul(self.ps[:, 4 + g, :], self.onesb,
                                                                sqh[k][:, g2 * 512:(g2 + 1) * 512],
                                                                start=(i == 0), stop=(i == nt - 1)),
                                 reads=[Rsqh[k], self.Rc], writes=[self.PS[4 + g]], inc=True)
                    pend_ss = ss_mm
                    hq += 1
            pend_ss()
            pend_ss = None
            S.op(S.act, lambda: nc.scalar.activation(out=acc[0], in_=self.psf(4, 4), func=AF.Ln, scale=1.0 / dim,
                                                     bias=self.epsb), reads=self.PS[4:8] + [self.Rc], writes=[Racc[0]])
            S.op(S.act, lambda: nc.scalar.activation(out=acc[0], in_=acc[0], func=AF.Exp, scale=-0.5),
                 reads=[Racc[0]], writes=[Racc[0]])
            for i in range(nt):
                S.op(S.dve, lambda: nc.vector.scalar_tensor_tensor(
                    out=dst[:, i, :], in0=dst[:, i, :], scalar=self.pvec[:, l, ncol + i:ncol + i + 1], in1=acc[0],
                    op0=ALU.mult, op1=ALU.mult), reads=[Rlat, Racc[0], self.Rc], writes=[Rlat])
        proj_tile(17, 0)
        pq = self.PS[0:4]
        pf = self.psf(0, 4)
        S.op(S.dve, lambda: nc.vector.tensor_tensor(out=acc[0][0:64, :], in0=pf[0:64, :], in1=self.cosT[0:64, :],
                                                     op=ALU.mult), reads=pq + [self.Rrope], writes=[Racc[0]])
        S.op(S.dve, lambda: nc.vector.tensor_tensor(out=acc[1][0:64, :], in0=pf[64:128, :], in1=self.sinT[64:128, :],
                                                     op=ALU.mult), reads=pq + [self.Rrope], writes=[Racc[1]])
        S.op(S.pool, lambda: nc.gpsimd.tensor_tensor(out=krT[0:64, :], in0=acc[0][0:64, :], in1=acc[1][0:64, :],
                                                     op=ALU.add), reads=[Racc[0], Racc[1]], writes=[Rlat])

    def ssd(self, b, l, hT, RhT, xbcT, Rxbc, wz, Rwz, st):
        nc, S = self.nc, self.S
        ps, PS = self.ps, self.PS

        def t2(name, shape, dt, n=2):
            return [(self.sb(st, "ssd_%s%d" % (name, i), shape, dt), S.res("ssd_%s%d" % (name, i))) for i in range(n)]

        def t1(name, shape, dt):
            return self.sb(st, "ssd_" + name, shape, dt), S.res("ssd_" + name)

        szs = t2("sz", [128, 1024], BF16)
        xss = t2("xs", [128, 16, 64], BF16)
        xdts = t2("xdt", [128, 16, 64], BF16)
        eseg, Reseg = t1("eseg", [128, 16, 128], BF16)
        MTa, RMTa = t1("MT", [128, 16, 128], BF16)
        dts = t2("dt", [128, 16], F32)
        as_ = t2("a", [128, 16], F32)
        acss = t2("acs", [128, 48], F32)
        Es = t2("E", [128, 16], F32)
        dtes = t2("dte", [128, 16], F32)
        cdecs = t2("cdec", [128, 16], F32)
        rshs = t2("rsh", [128, 16, 128], BF16)
        rsls = t2("rsl", [128, 16, 128], BF16)
        ahis = t2("ahi", [128, 16], BF16)
        alos = t2("alo", [128, 16], BF16)
        xde, Rxde = t1("xde", [128, 16, 64], BF16)
        Btok, RBtok = t1("Btok", [128, 256], BF16)
        cbm, Rcbm = t1("cbm", [128, 2, 128], BF16)
        yt, Ryt = t1("yt", [128, 16, 64], F32)
        y2, Ry2 = t1("y2", [128, 16, 64], F32)
        ssq, Rssq = t1("ssq", [128, 2], F32)
        junk, Rjunk = t1("junk", [128, 512], BF16)
        yh, Ryh = t1("yh", [128, 1024], BF16)
        state, Rstate = t1("state", [128, 16, 64], F32)
        prev, Rprev = t1("prev", [128, 1024], BF16)
        dtb = self.hvec[:, l, 0:16]
        dsk = self.hvec[:, l, 32:48]
        ah = self.ahead[:, l, :]
        G = [0, 1, 2, 3]
        KD, KZ, KX, KB = 0, [1, 2], 3, 4
        KO, KT, KS, KY = [4, 5], 5, [6, 7], [6, 7]

        S.op(S.dve, lambda: nc.vector.memset(state, 0.0), writes=[Rstate])
        S.op(S.dve, lambda: nc.vector.memset(prev, 0.0), writes=[Rprev])

        def stepP(c):
            s = c % 2
            cs = slice(c * 128, (c + 1) * 128)
            dt_, Rdt = dts[s]
            a_, Ra = as_[s]
            acs, Racs = acss[s]
            E_, RE = Es[s]
            dte, Rdte = dtes[s]
            cdec, Rcdec = cdecs[s]
            rsh, Rrsh = rshs[s]
            rsl, Rrsl = rsls[s]
            ahi, Rahi = ahis[s]
            alo, Ralo = alos[s]
            for kc in range(8):
                S.op(S.pe, lambda: nc.tensor.matmul(ps[:, KD, 0:16], hT[:, kc, cs], wz[:, kc, 1024:1040],
                                                    start=(kc == 0), stop=(kc == 7)),
                     reads=[RhT[c], Rwz], writes=[PS[KD]], inc=(kc == 7))
            S.op(S.dve, lambda: nc.vector.tensor_tensor(out=dt_, in0=ps[:, KD, 0:16], in1=dtb, op=ALU.add),
                 reads=[PS[KD], self.Rc], writes=[Rdt])
            S.op(S.act, lambda: nc.scalar.activation(out=dt_, in_=dt_, func=AF.Exp), reads=[Rdt], writes=[Rdt])
            S.op(S.act, lambda: nc.scalar.activation(out=dt_, in_=dt_, func=AF.Ln, bias=1.0), reads=[Rdt],
                 writes=[Rdt])
            S.op(S.dve, lambda: nc.vector.tensor_tensor(out=a_, in0=dt_, in1=ah, op=ALU.mult),
                 reads=[Rdt, self.Rc], writes=[Ra])
            S.op(S.dve, lambda: nc.vector.tensor_copy(ahi, a_), reads=[Ra], writes=[Rahi])
            S.op(S.dve, lambda: nc.vector.tensor_tensor(out=alo, in0=a_, in1=ahi, op=ALU.subtract),
                 reads=[Ra, Rahi], writes=[Ralo])
            S.op(S.dve, lambda: nc.vector.tensor_tensor(out=rsh, in0=ahi.unsqueeze(2).to_broadcast([128, 16, 128]),
                                                         in1=self.Lb.unsqueeze(1).to_broadcast([128, 16, 128]),
                                                         op=ALU.mult), reads=[Rahi, self.Rc], writes=[Rrsh])
            S.op(S.dve, lambda: nc.vector.tensor_tensor(out=rsl, in0=alo.unsqueeze(2).to_broadcast([128, 16, 128]),
                                                         in1=self.Lb.unsqueeze(1).to_broadcast([128, 16, 128]),
                                                         op=ALU.mult), reads=[Ralo, self.Rc], writes=[Rrsl])
            S.op(S.pe, lambda: nc.tensor.matmul(ps[:, KD, 16:32], self.Lf, a_, start=True, stop=True),
                 reads=[Ra, self.Rc], writes=[PS[KD]], inc=False)
            S.op(S.pe, lambda: nc.tensor.matmul(ps[:, KD, 32:48], self.onesf, a_, start=True, stop=True),
                 reads=[Ra, self.Rc], writes=[PS[KD]], inc=True)
            S.op(S.act, lambda: nc.scalar.copy(acs[:, 0:32], ps[:, KD, 16:48]), reads=[PS[KD]], writes=[Racs])
            S.op(S.act, lambda: nc.scalar.activation(out=E_, in_=acs[:, 0:16], func=AF.Exp), reads=[Racs], writes=[RE])
            S.op(S.dve, lambda: nc.vector.tensor_tensor(out=acs[:, 32:48], in0=acs[:, 16:32], in1=acs[:, 0:16],
                                                         op=ALU.subtract), reads=[Racs], writes=[Racs])
            S.op(S.act, lambda: nc.scalar.activation(out=dte, in_=acs[:, 32:48], func=AF.Exp), reads=[Racs],
                 writes=[Rdte])
            S.op(S.act, lambda: nc.scalar.activation(out=cdec, in_=acs[:, 16:32], func=AF.Exp), reads=[Racs],
                 writes=[Rcdec])

        def stepA(c):
            s = c % 2
            rsh, Rrsh = rshs[s]
            rsl, Rrsl = rsls[s]
            for q in range(4):
                S.op(S.pe, lambda: nc.tensor.matmul(ps[:, G[q], :], self.Ub,
                                                    rsh[:, q * 4:q * 4 + 4, :].rearrange("p h l -> p (h l)"),
                                                    start=True, stop=False),
                     reads=[Rrsh, self.Rc], writes=[PS[G[q]]], inc=False)
                S.op(S.pe, lambda: nc.tensor.matmul(ps[:, G[q], :], self.Ub,
                                                    rsl[:, q * 4:q * 4 + 4, :].rearrange("p h l -> p (h l)"),
                                                    start=False, stop=True),
                     reads=[Rrsl, self.Rc], writes=[PS[G[q]]], inc=True)
            S.op(S.act, lambda: nc.scalar.activation(out=eseg.rearrange("p h l -> p (h l)"), in_=self.psf(0, 4),
                                                     func=AF.Exp), reads=PS[0:4], writes=[Reseg])
            for hh in range(2):
                S.op(S.dve, lambda: nc.vector.tensor_tensor(out=MTa[:, hh * 8:(hh + 1) * 8, :],
                                                             in0=eseg[:, hh * 8:(hh + 1) * 8, :],
                                                             in1=cbm[:, hh:hh + 1, :].to_broadcast([128, 8, 128]),
                                                             op=ALU.mult), reads=[Reseg, Rcbm], writes=[RMTa])

        def stepB(c):
            s = c % 2
            cs = slice(c * 128, (c + 1) * 128)
            E_, RE = Es[s]
            cdec, Rcdec = cdecs[s]
            for hh in range(2):
                S.op(S.pe, lambda: nc.tensor.matmul(ps[:, KO[hh], :], xbcT[:, 10 + hh, cs],
                                                    prev[:, hh * 512:(hh + 1) * 512], start=True, stop=True),
                     reads=[Rxbc, Rprev], writes=[PS[KO[hh]]], inc=True)
            for g in range(2):
                S.op(S.pe, lambda: nc.tensor.matmul(ps[:, KS[g], :], Btok[:, g * 128:(g + 1) * 128],
                                                    xde[:, g * 8:(g + 1) * 8, :].rearrange("p h d -> p (h d)"),
                                                    start=True, stop=True),
                     reads=[RBtok, Rxde], writes=[PS[KS[g]]], inc=True)
            S.op(S.dve, lambda: nc.vector.tensor_tensor(out=yt, in0=self.psf(KO[0], 2).rearrange("p (h d) -> p h d", h=16),
                                                         in1=E_.unsqueeze(2).to_broadcast([128, 16, 64]), op=ALU.mult),
                 reads=[PS[KO[0]], PS[KO[1]], RE], writes=[Ryt])
            S.op(S.dve, lambda: nc.vector.tensor_tensor(out=state, in0=state,
                                                         in1=cdec.unsqueeze(2).to_broadcast([128, 16, 64]), op=ALU.mult),
                 reads=[Rstate, Rcdec], writes=[Rstate])
            S.op(S.dve, lambda: nc.vector.tensor_tensor(out=state,
                                                         in0=self.psf(KS[0], 2).rearrange("p (h d) -> p h d", h=16),
                                                         in1=state, op=ALU.add),
                 reads=[PS[KS[0]], PS[KS[1]], Rstate], writes=[Rstate])

        def stepC(c):
            s = c % 2
            cs = slice(c * 128, (c + 1) * 128)
            Rh = RhT[c]
            sz, Rsz = szs[s]
            xs, Rxs = xss[s]
            xdt, Rxdt = xdts[s]
            dt_, Rdt = dts[s]
            dte, Rdte = dtes[s]
            for n in range(2):
                for kc in range(8):
                    S.op(S.pe, lambda: nc.tensor.matmul(ps[:, KZ[n], :], hT[:, kc, cs], wz[:, kc, n * 512:(n + 1) * 512],
                                                        start=(kc == 0), stop=(kc == 7)),
                         reads=[Rh, Rwz], writes=[PS[KZ[n]]], inc=(kc == 7))
            for k in range(8):
                S.op(S.pe, lambda: nc.tensor.transpose(self.psb(KX)[:, k * 128:(k + 1) * 128], xbcT[:, k, cs],
                                                       self.identb), reads=[Rxbc, self.Rc], writes=[PS[KX]],
                     inc=(k == 7))
            for k in range(2):
                S.op(S.pe, lambda: nc.tensor.transpose(self.psb(KB)[:, k * 128:(k + 1) * 128], xbcT[:, 8 + k, cs],
                                                       self.identb), reads=[Rxbc, self.Rc], writes=[PS[KB]], inc=False)
            for g in range(2):
                S.op(S.pe, lambda: nc.tensor.matmul(ps[:, KB, 128 + g * 128:256 + g * 128], xbcT[:, 8 + g, cs],
                                                    xbcT[:, 10 + g, cs], start=True, stop=True),
                     reads=[Rxbc], writes=[PS[KB]], inc=(g == 1))
            for n in range(2):
                S.op(S.act, lambda: nc.scalar.activation(out=sz[:, n * 512:(n + 1) * 512], in_=ps[:, KZ[n], :],
                                                         func=AF.Silu), reads=[PS[KZ[n]]], writes=[Rsz])
            xsp = self.psb(KX).rearrange("p (h d) -> p h d", h=16)
            S.op(S.act, lambda: nc.scalar.copy(xs, xsp), reads=[PS[KX]], writes=[Rxs])
            S.op(S.dve, lambda: nc.vector.tensor_tensor(out=xdt, in0=xsp, in1=dt_.unsqueeze(2).to_broadcast([128, 16, 64]),
                                                         op=ALU.mult), reads=[PS[KX], Rdt], writes=[Rxdt])
            S.op(S.act, lambda: nc.scalar.copy(Btok, self.psb(KB)[:, 0:256]), reads=[PS[KB]], writes=[RBtok])
            S.op(S.dve, lambda: nc.vector.tensor_tensor(out=cbm, in0=ps[:, KB, 128:384].rearrange("p (g l) -> p g l", g=2),
                                                         in1=self.Lf.unsqueeze(1).to_broadcast([128, 2, 128]),
                                                         op=ALU.mult), reads=[PS[KB], self.Rc], writes=[Rcbm])
            S.op(S.dve, lambda: nc.vector.tensor_tensor(out=xde, in0=xdt,
                                                         in1=dte.unsqueeze(2).to_broadcast([128, 16, 64]), op=ALU.mult),
                 reads=[Rxdt, Rdte], writes=[Rxde])

        def stepD(c):
            cs = slice(c * 128, (c + 1) * 128)
            for k in range(8):
                S.op(S.pe, lambda: nc.tensor.transpose(self.psb(KT)[:, k * 128:(k + 1) * 128], yh[:, k * 128:(k + 1) * 128],
                                                       self.identb), reads=[Ryh, self.Rc], writes=[PS[KT]], inc=(k == 7))
            S.op(S.act, lambda: nc.scalar.copy(hT[:, :, cs], self.psb(KT).rearrange("p (k t) -> p k t", k=8)),
                 reads=[PS[KT]], writes=[RhT[c]])

        def stepE1(c):
            s = c % 2
            sz, Rsz = szs[s]
            xs, Rxs = xss[s]
            xdt, Rxdt = xdts[s]
            for hh in range(2):
                for jj in range(8):
                    S.op(S.pe, lambda: nc.tensor.matmul(ps[:, KY[hh], jj * 64:(jj + 1) * 64], MTa[:, hh * 8 + jj, :],
                                                        xdt[:, hh * 8 + jj, :], start=True, stop=True),
                         reads=[RMTa, Rxdt], writes=[PS[KY[hh]]], inc=(jj == 7))
            S.op(S.dve, lambda: nc.vector.tensor_tensor(out=y2, in0=xs, in1=dsk.unsqueeze(2).to_broadcast([128, 16, 64]),
                                                         op=ALU.mult), reads=[Rxs, self.Rc], writes=[Ry2])
            S.op(S.dve, lambda: nc.vector.tensor_tensor(out=yt, in0=self.psf(KY[0], 2).rearrange("p (h d) -> p h d", h=16),
                                                         in1=yt, op=ALU.add),
                 reads=[PS[KY[0]], PS[KY[1]], Ryt], writes=[Ryt])
            S.op(S.dve, lambda: nc.vector.tensor_tensor(out=y2, in0=y2, in1=yt, op=ALU.add), reads=[Ry2, Ryt],
                 writes=[Ry2])
            y2f = y2.rearrange("p h d -> p (h d)")
            S.op(S.dve, lambda: nc.vector.tensor_tensor(out=y2f, in0=y2f, in1=sz, op=ALU.mult), reads=[Ry2, Rsz],
                 writes=[Ry2])
            S.op(S.act, lambda: nc.scalar.copy(prev, state.rearrange("p h d -> p (h d)")), reads=[Rstate],
                 writes=[Rprev])

        def stepE2(c):
            y2f = y2.rearrange("p h d -> p (h d)")
            for g in range(2):
                S.op(S.act, lambda: nc.scalar.activation(out=junk, in_=y2f[:, g * 512:(g + 1) * 512], func=AF.Square,
                                                         accum_out=ssq[:, g:g + 1]), reads=[Ry2],
                     writes=[Rjunk, Rssq])
            S.op(S.act, lambda: nc.scalar.activation(out=ssq, in_=ssq, func=AF.Ln, scale=1.0 / 512, bias=self.epsb),
                 reads=[Rssq, self.Rc], writes=[Rssq])
            S.op(S.act, lambda: nc.scalar.activation(out=ssq, in_=ssq, func=AF.Exp, scale=-0.5),
                 reads=[Rssq], writes=[Rssq])
            for g in range(2):
                S.op(S.act, lambda: nc.scalar.activation(out=yh[:, g * 512:(g + 1) * 512],
                                                         in_=y2f[:, g * 512:(g + 1) * 512], func=AF.Copy,
                                                         scale=ssq[:, g:g + 1]), reads=[Ry2, Rssq], writes=[Ryh])

        stepP(0)
        stepP(1)
        stepC(0)
        for c in range(NCH):
            stepA(c)
            if c >= 1:
                stepE2(c - 1)
            stepB(c)
            if c + 2 < NCH:
                stepP(c + 2)
            if c + 1 < NCH:
                stepC(c + 1)
            if c >= 1:
                stepD(c - 1)
            stepE1(c)
        stepE2(NCH - 1)
        stepD(NCH - 1)

    def attention(self, b, l, cqn, ckvn, krT, Rlat, oT, RoT, st):
        nc, S = self.nc, self.S
        ps, PS = self.ps, self.PS
        vall = self.sb(st, "vall", [128, NCH, 1024], BF16)
        Rv = [S.res("vall0"), S.res("vall1")]
        if True:
            wv = self.sb(st, "wv", [128, 2, 1024], BF16)
            Rwv = S.res("wv")
            S.dma(S.pool, wv, self.wukv_d[l, :, :, 1024:2048], writes=[Rwv])
            for tkn in range(NCH):
                for n in range(2):
                    bank = (tkn * 2 + n) % 4
                    for kc in range(2):
                        S.op(S.pe, lambda: nc.tensor.matmul(ps[:, bank, :], ckvn[:, kc, tkn * 128:(tkn + 1) * 128],
                                                            wv[:, kc, n * 512:(n + 1) * 512],
                                                            start=(kc == 0), stop=(kc == 1)),
                             reads=[Rlat, Rwv], writes=[PS[bank]], inc=(kc == 1))
                    if n == 0:
                        S.op(S.act, lambda: nc.scalar.copy(vall[:, tkn, 0:512], ps[:, bank, :]), reads=[PS[bank]],
                             writes=[Rv[0]])
                    else:
                        S.op(S.dve, lambda: nc.vector.tensor_copy(vall[:, tkn, 512:1024], ps[:, bank, :]),
                             reads=[PS[bank]], writes=[Rv[1]])
        wqh = [self.sb(st, "wqh%d" % i, [128, 3, 256], BF16) for i in range(2)]
        wkh = [self.sb(st, "wkh%d" % i, [128, 2, 128], BF16) for i in range(2)]
        Rwh = [S.res("wh%d" % i) for i in range(2)]
        qn = [self.sb(st, "qn%d" % i, [128, SEQ], BF16) for i in range(2)]
        kn = [self.sb(st, "kn%d" % i, [128, SEQ], BF16) for i in range(2)]
        qr = [self.sb(st, "qr%d" % i, [128, SEQ], BF16) for i in range(2)]
        Rqn = [S.res("qn%d" % i) for i in range(2)]
        Rkn = [S.res("kn%d" % i) for i in range(2)]
        Rqr = [S.res("qr%d" % i) for i in range(2)]
        for i in range(2):
            S.op(S.pool, lambda: nc.gpsimd.memset(qr[i][64:128, :], 0.0), writes=[Rqr[i]])
        r1 = self.sb(st, "r1", [64, 512], F32)
        r2 = self.sb(st, "r2", [64, 512], F32)
        Rr1, Rr2 = S.res("r1"), S.res("r2")
        NP = 5
        PT = [self.sb(st, "PT%d" % i, [128, 512], BF16) for i in range(NP)]
        RPT = [S.res("PT%d" % i) for i in range(NP)]
        rs = self.sb(st, "rs", [128, 512], F32)
        Rrs = S.res("rs")

        def load_w(h):
            i = h % 2
            S.dma(S.pool, wqh[i], self.wuq_d[l, :, :, h * 256:(h + 1) * 256], writes=[Rwh[i]])
            S.dma(S.pool, wkh[i], self.wukv_d[l, :, :, h * 128:(h + 1) * 128], writes=[Rwh[i]])

        def proj_pieces(h):
            i = h % 2
            pieces = []
            for g in range(NG):
                gs = slice(g * 512, (g + 1) * 512)

                def p_qn(gs=gs):
                    for kc in range(3):
                        S.op(S.pe, lambda: nc.tensor.matmul(ps[:, 0, :], wqh[i][:, kc, 0:128], cqn[:, kc, gs],
                                                            start=(kc == 0), stop=(kc == 2)),
                             reads=[Rlat, Rwh[i]], writes=[PS[0]], inc=(kc == 2))
                    S.op(S.act, lambda: nc.scalar.copy(qn[i][:, gs], ps[:, 0, :]), reads=[PS[0]], writes=[Rqn[i]])

                def p_qr(gs=gs):
                    for kc in range(3):
                        S.op(S.pe, lambda: nc.tensor.matmul(ps[:, 0, :], wqh[i][:, kc, 128:256], cqn[:, kc, gs],
                                                            start=(kc == 0), stop=(kc == 2)),
                             reads=[Rlat, Rwh[i]], writes=[PS[0]], inc=(kc == 2))
                    S.op(S.dve, lambda: nc.vector.tensor_tensor(out=r1, in0=ps[0:64, 0, :], in1=self.cosT[0:64, gs],
                                                                 op=ALU.mult), reads=[PS[0], self.Rrope], writes=[Rr1])
                    S.op(S.dve, lambda: nc.vector.tensor_tensor(out=r2, in0=ps[64:128, 0, :], in1=self.sinT[64:128, gs],
                                                                 op=ALU.mult), reads=[PS[0], self.Rrope], writes=[Rr2])
                    S.op(S.dve, lambda: nc.vector.tensor_tensor(out=qr[i][0:64, gs], in0=r1, in1=r2, op=ALU.add),
                         reads=[Rr1, Rr2], writes=[Rqr[i]])

                def p_kn(gs=gs):
                    for kc in range(2):
                        S.op(S.pe, lambda: nc.tensor.matmul(ps[:, 0, :], wkh[i][:, kc, :], ckvn[:, kc, gs],
                                                            start=(kc == 0), stop=(kc == 1)),
                             reads=[Rlat, Rwh[i]], writes=[PS[0]], inc=(kc == 1))
                    S.op(S.act, lambda: nc.scalar.copy(kn[i][:, gs], ps[:, 0, :]), reads=[PS[0]], writes=[Rkn[i]])

                pieces += [p_qn, p_qr, p_kn]
            return pieces

        load_w(0)
        load_w(1)
        for p in proj_pieces(0):
            p()
        pi = 0
        sc = 0
        for h in range(8):
            i = h % 2
            pieces = proj_pieces(h + 1) if h + 1 < 8 else []
            ucount = 0
            for j in range(NG):
                ob, sb_ = 4 + (j % 2) * 2, 5 + (j % 2) * 2
                units = list(range(4 * j + 4))
                pend = []
                for ui, kt in enumerate(units):
                    off = max(0, kt - 4 * j) * 128
                    qs = slice(j * 512 + off, (j + 1) * 512)
                    ks = slice(kt * 128, (kt + 1) * 128)
                    sbank = 1 + sc % 3
                    sc += 1
                    diag = kt >= 4 * j
                    S.op(S.pe, lambda: nc.tensor.matmul(ps[:, sbank, off:512], kn[i][:, ks], qn[i][:, qs], start=True,
                                                        stop=False), reads=[Rkn[i], Rqn[i]], writes=[PS[sbank]],
                         inc=False)
                    S.op(S.pe, lambda: nc.tensor.matmul(ps[:, sbank, off:512], krT[:, ks], qr[i][:, qs], start=False,
                                                        stop=not diag), reads=[Rlat, Rqr[i]], writes=[PS[sbank]],
                         inc=not diag)
                    if diag:
                        S.op(S.pe, lambda: nc.tensor.matmul(ps[:, sbank, off:off + 128], self.identb, self.negb,
                                                            start=False, stop=True), reads=[self.Rc],
                             writes=[PS[sbank]], inc=True)
                    if len(pend) >= 2:
                        self._pv(*pend.pop(0))
                    p = pi % NP
                    pi += 1
                    S.op(S.act, lambda: nc.scalar.activation(out=PT[p][:, off:512], in_=ps[:, sbank, off:512],
                                                             func=AF.Exp, scale=SM_SCALE), reads=[PS[sbank]],
                         writes=[RPT[p]])
                    pend.append((vall, Rv, h, kt, off, PT[p], RPT[p], ob, sb_, ui == 0, ui == len(units) - 1))
                    ucount += 1
                    if pieces and ucount % 3 == 0:
                        pieces.pop(0)()
                while pend:
                    self._pv(*pend.pop(0))
                S.op(S.act, lambda: nc.scalar.activation(out=rs, in_=ps[:, sb_, :], func=AF.Ln), reads=[PS[sb_]],
                     writes=[Rrs])
                S.op(S.act, lambda: nc.scalar.activation(out=rs, in_=rs, func=AF.Exp, scale=-1.0), reads=[Rrs],
                     writes=[Rrs])
                S.op(S.dve, lambda: nc.vector.tensor_tensor(out=oT[:, h, j * 512:(j + 1) * 512], in0=ps[:, ob, :], in1=rs,
                                                             op=ALU.mult), reads=[PS[ob], Rrs], writes=[RoT])
            while pieces:
                pieces.pop(0)()
            if h + 2 < 8:
                load_w(h + 2)

    def _pv(self, vall, Rv, h, kt, off, PTp, RPTp, ob, sb_, first, last):
        nc, S = self.nc, self.S
        ps, PS = self.ps, self.PS
        S.op(S.pe, lambda: nc.tensor.matmul(ps[:, ob, off:512], vall[:, kt, h * 128:(h + 1) * 128], PTp[:, off:512],
                                            start=first, stop=last), reads=Rv + [RPTp], writes=[PS[ob]], inc=False)
        S.op(S.pe, lambda: nc.tensor.matmul(ps[:, sb_, off:512], self.onesb, PTp[:, off:512], start=first, stop=last),
             reads=[self.Rc, RPTp], writes=[PS[sb_]], inc=True)

    def out_proj(self, b, l, yT, oT, RoT, st):
        nc, S = self.nc, self.S
        ps, PS = self.ps, self.PS
        NQ = 2
        wo = [self.sb(st, "wo%d" % i, [128, 16, 512], BF16) for i in range(NQ)]
        Rwo = [S.res("wo%d" % i) for i in range(NQ)]
        for i in range(NQ):
            S.dma(S.pool, wo[i], self.wout_d[l, :, :, i * 512:(i + 1) * 512], writes=[Rwo[i]])
        sq = self.sb(st, "op_sq", [128, 8, 512], BF16)
        Rsq = S.res("op_sq")
        rstd = self.sb(st, "op_rstd", [128, 512], F32)
        Rrstd = S.res("op_rstd")
        RyT = S.res("yT_ro")
        def stats(g):
            gs = slice(g * 512, (g + 1) * 512)
            S.op(S.act, lambda: nc.scalar.activation(out=sq, in_=oT[:, :, gs], func=AF.Square), reads=[RoT],
                 writes=[Rsq])
            for kc in range(8):
                S.op(S.pe, lambda: nc.tensor.matmul(ps[:, g % 2, :], self.onesb, sq[:, kc, :], start=(kc == 0),
                                                    stop=(kc == 7)), reads=[Rsq, self.Rc], writes=[PS[g % 2]],
                     inc=(kc == 7))

        stats(0)
        for g in range(NG):
            gs = slice(g * 512, (g + 1) * 512)
            bank = g % 2
            S.op(S.act, lambda: nc.scalar.activation(out=rstd, in_=ps[:, bank, :], func=AF.Ln, scale=1.0 / D,
                                                     bias=self.epsb), reads=[PS[bank], self.Rc], writes=[Rrstd])
            S.op(S.act, lambda: nc.scalar.activation(out=rstd, in_=rstd, func=AF.Exp, scale=-0.5), reads=[Rrstd],
                 writes=[Rrstd])
            if g + 1 < NG:
                stats(g + 1)
            S.op(S.dve, lambda: nc.vector.tensor_tensor(out=oT[:, :, gs], in0=oT[:, :, gs],
                                                         in1=rstd.unsqueeze(1).to_broadcast([128, 8, 512]), op=ALU.mult),
                 reads=[RoT, Rrstd], writes=[RoT])
        for i in range(NQ):
            for kc in range(16):
                col = (PV["ssd_norm"] + kc) if kc < 8 else (PV["attn_norm"] + kc - 8)
                S.op(S.dve, lambda: nc.vector.tensor_scalar(out=wo[i][:, kc, :], in0=wo[i][:, kc, :],
                                                             scalar1=self.pvec[:, l, col:col + 1], scalar2=None,
                                                             op0=ALU.mult),
                     reads=[Rwo[i], self.Rc], writes=[Rwo[i]])
        self.dump("oTn", oT, [RoT], [128, 8, SEQ], BF16)
        self.resid_proj(b, l, lambda kc: (yT[:, kc, :] if kc < 8 else oT[:, kc - 8, :]), [RyT, RoT], 16, wo, Rwo, 16, st)

    def resid_proj(self, b, l, src, Rsrc, nk, w, Rw, gate0, st):
        nc, S = self.nc, self.S
        ps, PS = self.ps, self.PS
        xm = [self.sb(st, "rp_xm%d" % i, [128, SEQ], F32) for i in range(2)]
        Rxm = [S.res("rp_xm%d" % i) for i in range(2)]

        def ld(m):
            S.dma(S.sp, xm[m % 2], self.xres_d[b, :, m, :], reads=[self.Rxres[b][m]], writes=[Rxm[m % 2]])

        ld(0)
        bi = 0
        for m in range(8):
            if m + 1 < 8:
                ld(m + 1)
            per = 8 // len(w)
            wi, mo = m // per, (m % per) * 128
            for g in range(NG):
                bank = 4 + bi % 4
                bi += 1
                for kc in range(nk):
                    S.op(S.pe, lambda: nc.tensor.matmul(ps[:, bank, :], w[wi][:, kc, mo:mo + 128],
                                                        src(kc)[:, g * 512:(g + 1) * 512], start=(kc == 0),
                                                        stop=(kc == nk - 1)),
                         reads=[Rw[wi]] + Rsrc, writes=[PS[bank]], inc=(kc == nk - 1))
                S.op(S.dve, lambda: nc.vector.scalar_tensor_tensor(
                    out=xm[m % 2][:, g * 512:(g + 1) * 512], in0=ps[:, bank, :],
                    scalar=self.modT[:, l, gate0 + m, b:b + 1], in1=xm[m % 2][:, g * 512:(g + 1) * 512],
                    op0=ALU.mult, op1=ALU.add), reads=[PS[bank], self.Rmod, Rxm[m % 2]], writes=[Rxm[m % 2]])
            S.dma(S.sp, self.xres_d[b, :, m, :], xm[m % 2], reads=[Rxm[m % 2]], writes=[self.Rxres[b][m]])

    def ffn(self, b, l):
        nc, S = self.nc, self.S
        ps, PS = self.ps, self.PS
        with ExitStack() as fst:
            h2 = self.sb(fst, "h2T", [128, 8, SEQ], BF16)
            Rh2 = S.res("h2T")
            with ExitStack() as st:
                self.normmod(b, l, h2, Rh2, PV["norm_mlp"], 24, 32, st)
                S.barrier()
            self.dump("h2T", h2, [], [128, 8, SEQ], BF16)
            actT = self.sb(fst, "actT", [128, 22, SEQ], BF16)
            Ract = S.res("actT")
            wd0 = self.sb(fst, "wd0", [128, 22, 512], BF16)
            Rwd0 = S.res("wd0")
            S.dma(S.pool, wd0, self.wdn_d[l, :, :, 0:512], writes=[Rwd0])
            with ExitStack() as st:
                wu = [self.sb(st, "wu%d" % i, [128, 8, 512], BF16) for i in range(2)]
                Rwu = [S.res("wu%d" % i) for i in range(2)]
                acc = [self.sb(st, "facc%d" % i, [128, SEQ], F32) for i in range(2)]
                Racc = [S.res("facc%d" % i) for i in range(2)]
                gsl = self.sb(st, "gsl", [128, SEQ], BF16)
                Rgsl = S.res("gsl")
                fw, fb = PV["ffw"], PV["ffb"]

                def ldw(blk):
                    S.dma(S.pool, wu[blk % 2], self.wup_d[l, :, :, blk * 512:(blk + 1) * 512], writes=[Rwu[blk % 2]])

                ldw(0)
                for j in range(22):
                    blk = j // 2
                    if j % 2 == 0 and blk + 1 < 11:
                        ldw(blk + 1)
                    for q in range(2):
                        t = j if q == 0 else 22 + j
                        mo = (j % 2) * 256 + q * 128
                        for g in range(NG):
                            for kc in range(8):
                                S.op(S.pe, lambda: nc.tensor.matmul(ps[:, q * 4 + g, :], wu[blk % 2][:, kc, mo:mo + 128],
                                                                    h2[:, kc, g * 512:(g + 1) * 512], start=(kc == 0),
                                                                    stop=(kc == 7)),
                                     reads=[Rwu[blk % 2], Rh2], writes=[PS[q * 4 + g]], inc=(kc == 7))
                        pq = PS[q * 4:q * 4 + 4]
                        pf = self.psf(q * 4, 4)
                        a = acc[q]
                        S.op(S.act, lambda: nc.scalar.activation(out=a, in_=pf, func=AF.Identity,
                                                                 scale=self.pvec[:, l, fw + t * 3 + 2:fw + t * 3 + 3],
                                                                 bias=self.pvec[:, l, fb + t:fb + t + 1]),
                             reads=pq + [self.Rc], writes=[Racc[q]])
                        for k in range(2):
                            sh = 2 - k
                            S.op(S.dve, lambda: nc.vector.scalar_tensor_tensor(
                                out=a[:, sh:SEQ], in0=pf[:, 0:SEQ - sh],
                                scalar=self.pvec[:, l, fw + t * 3 + k:fw + t * 3 + k + 1], in1=a[:, sh:SEQ],
                                op0=ALU.mult, op1=ALU.add), reads=pq + [self.Rc, Racc[q]], writes=[Racc[q]])
                        if q == 0:
                            S.op(S.act, lambda: nc.scalar.activation(out=gsl, in_=a, func=AF.Silu), reads=[Racc[0]],
                                 writes=[Rgsl])
                    S.op(S.pool, lambda: nc.gpsimd.tensor_tensor(out=actT[:, j, :], in0=gsl, in1=acc[1], op=ALU.mult),
                         reads=[Rgsl, Racc[1]], writes=[Ract])
                S.barrier()
            with ExitStack() as st:
                wd = [wd0, self.sb(st, "wd1", [128, 22, 512], BF16)]
                Rwd = [Rwd0, S.res("wd1")]
                S.dma(S.pool, wd[1], self.wdn_d[l, :, :, 512:1024], writes=[Rwd[1]])
                self.resid_proj(b, l, lambda kc: actT[:, kc, :], [Ract], 22, wd, Rwd, 40, st)
                S.barrier()

    def final(self, b):
        nc, S = self.nc, self.S
        ps, PS = self.ps, self.PS
        with ExitStack() as st:
            xg = [self.sb(st, "fn_xg%d" % i, [128, 8, 512], F32) for i in range(3)]
            Rxg = [S.res("fn_xg%d" % i) for i in range(3)]
            sq = self.sb(st, "fn_sq", [128, 8, 512], BF16)
            Rsq = S.res("fn_sq")
            rstd = self.sb(st, "fn_rstd", [128, 512], F32)
            Rrstd = S.res("fn_rstd")
            ot = [self.sb(st, "fn_ot%d" % i, [128, D], F32) for i in range(2)]
            Rot = [S.res("fn_ot%d" % i) for i in range(2)]
            Rxk = [[S.res("fn_xk%d_%d" % (i, kc)) for kc in range(8)] for i in range(3)]

            def ld(g):
                S.dma(S.sp, xg[g % 3], self.xres_d[b, :, :, g * 512:(g + 1) * 512], reads=self.Rxres[b],
                      writes=[Rxg[g % 3]] + Rxk[g % 3])

            def stats(g):
                i = g % 3
                S.op(S.act, lambda: nc.scalar.activation(out=sq, in_=xg[i], func=AF.Square), reads=[Rxg[i]],
                     writes=[Rsq])
                for kc in range(8):
                    S.op(S.pe, lambda: nc.tensor.matmul(ps[:, g % 2, :], self.onesb, sq[:, kc, :], start=(kc == 0),
                                                        stop=(kc == 7)), reads=[Rsq, self.Rc], writes=[PS[g % 2]],
                         inc=(kc == 7))

            ld(0)
            ld(1)
            ld(2)
            stats(0)
            oi = 0
            for g in range(NG):
                i = g % 3
                S.op(S.act, lambda: nc.scalar.activation(out=rstd, in_=ps[:, g % 2, :], func=AF.Ln, scale=1.0 / D,
                                                         bias=self.epsb), reads=[PS[g % 2], self.Rc], writes=[Rrstd])
                S.op(S.act, lambda: nc.scalar.activation(out=rstd, in_=rstd, func=AF.Exp, scale=-0.5), reads=[Rrstd],
                     writes=[Rrstd])
                S.op(S.dve, lambda: nc.vector.tensor_tensor(out=xg[i], in0=xg[i],
                                                             in1=rstd.unsqueeze(1).to_broadcast([128, 8, 512]),
                                                             op=ALU.mult), reads=[Rrstd, Rxg[i]], writes=[Rxg[i]])
                if g + 1 < NG:
                    stats(g + 1)
                for kc in range(8):
                    if kc % 2 == 0:
                        S.op(S.act, lambda: nc.scalar.activation(out=xg[i][:, kc, :], in_=xg[i][:, kc, :], func=AF.Copy,
                                                                 scale=self.fnorm[:, kc:kc + 1]),
                             reads=[Rxg[i], self.Rc], writes=[Rxk[i][kc]])
                    else:
                        S.op(S.dve, lambda: nc.vector.tensor_scalar(out=xg[i][:, kc, :], in0=xg[i][:, kc, :],
                                                                     scalar1=self.fnorm[:, kc:kc + 1], scalar2=None,
                                                                     op0=ALU.mult),
                             reads=[Rxg[i], self.Rc], writes=[Rxk[i][kc]])
                for t in range(4):
                    o = oi % 2
                    oi += 1
                    for half in range(2):
                        for kq in range(4):
                            kc = half * 4 + kq
                            bank = 4 + (t % 2) * 2 + half
                            S.op(S.pe, lambda: nc.tensor.transpose(ps[:, bank, kq * 128:(kq + 1) * 128],
                                                                   xg[i][:, kc, t * 128:(t + 1) * 128], self.identf),
                                 reads=[Rxk[i][kc], self.Rc], writes=[PS[bank]], inc=(kq == 3))
                    b0 = 4 + (t % 2) * 2
                    S.op(S.act if t % 2 == 0 else S.dve,
                         (lambda: nc.scalar.copy(ot[o], self.psf(b0, 2))) if t % 2 == 0 else
                         (lambda: nc.vector.tensor_copy(ot[o], self.psf(b0, 2))),
                         reads=PS[b0:b0 + 2], writes=[Rot[o]])
                    r0 = (g * 4 + t) * 128
                    S.dma(S.sp, self.out_d[b, r0:r0 + 128, :], ot[o], reads=[Rot[o]], writes=[self.Rout])
                if g + 3 < NG:
                    ld(g + 3)


def _pk(v):
    v = np.asarray(v, np.float32)
    return np.ascontiguousarray(v.reshape(-1, 128).T)


def _wl(w):
    k, n = w.shape
    return np.ascontiguousarray(w.reshape(k // 128, 128, n).transpose(1, 0, 2))


def _prep_shared(inp):
    L = DEPTH
    f = lambda k: np.asarray(inp[k], np.float32)
    pvec = np.zeros((128, L, NPV), np.float32)
    hvec = np.zeros((128, L, 48), np.float32)
    w_in, w_uq, w_ukv, w_up = [], [], [], []
    for l in range(L):
        pvec[:, l, PV["norm_mix"]:PV["norm_mix"] + 8] = _pk(f("norm_mix")[l])
        cw = f("conv_w")[l]
        pvec[:, l, PV["conv_w"]:PV["conv_w"] + 48] = cw.reshape(4, 12, 128).transpose(2, 1, 0).reshape(128, 48)
        pvec[:, l, PV["conv_b"]:PV["conv_b"] + 12] = _pk(f("conv_b")[l])
        pvec[:, l, PV["q_norm"]:PV["q_norm"] + 3] = _pk(f("q_norm")[l])
        pvec[:, l, PV["kv_norm"]:PV["kv_norm"] + 2] = _pk(f("kv_norm")[l])
        pvec[:, l, PV["attn_norm"]:PV["attn_norm"] + 8] = _pk(f("attn_norm")[l])
        pvec[:, l, PV["ssd_norm"]:PV["ssd_norm"] + 8] = _pk(f("ssd_norm")[l])
        pvec[:, l, PV["norm_mlp"]:PV["norm_mlp"] + 8] = _pk(f("norm_mlp")[l])
        fw = f("conv_ff_w")[l]
        pvec[:, l, PV["ffw"]:PV["ffw"] + 132] = fw.reshape(3, 44, 128).transpose(2, 1, 0).reshape(128, 132)
        pvec[:, l, PV["ffb"]:PV["ffb"] + 44] = _pk(f("conv_ff_b")[l])
        pvec[:, l, PV["b_ada"]:PV["b_ada"] + 48] = _pk(f("b_ada")[l])
        hvec[:, l, 0:16] = f("dt_bias")[l][None, :]
        hvec[:, l, 16:32] = f("a_log")[l][None, :]
        hvec[:, l, 32:48] = f("d_skip")[l][None, :]
        wi = f("w_in")[l]
        kr = wi[:, 3216:3280]
        ext = np.concatenate([wi[:, 1024:2560], wi[:, 2576:2960], wi[:, 2960:3216], kr,
                              kr[:, 32:64], kr[:, 0:32], wi[:, 0:1024], wi[:, 2560:2576]], axis=1)
        w_in.append(_wl(ext))
        wq = f("w_uq")[l].reshape(384, 8, 192)
        wq_ext = np.concatenate([wq[:, :, 0:128], wq[:, :, 128:192], wq[:, :, 160:192], wq[:, :, 128:160]], axis=2)
        w_uq.append(_wl(wq_ext.reshape(384, 2048)))
        wk = f("w_ukv")[l].reshape(256, 8, 256)
        w_ukv.append(_wl(np.concatenate([wk[:, :, 0:128].reshape(256, 1024), wk[:, :, 128:256].reshape(256, 1024)],
                                        axis=1)))
        wu = f("w_up")[l]
        wu_p = np.stack([wu[:, 0:D_FF].reshape(D, 22, 128), wu[:, D_FF:].reshape(D, 22, 128)], axis=2)
        w_up.append(_wl(wu_p.reshape(D, 2 * D_FF)))
    consts = np.zeros((128, 4, 128), np.float32)
    i = np.arange(128)
    consts[:, 0, :] = np.eye(128)
    consts[:, 1, :] = (i[:, None] <= i[None, :])
    consts[:, 2, :] = (i[:, None] > i[None, :])
    consts[:, 3, :] = 1.0
    inv_freq = (1.0 / (10000.0 ** (np.arange(0, 64, 2, dtype=np.float32) / 64))).astype(np.float32)
    ropec = np.zeros((128, 2), np.float32)
    ropec[:, 0] = np.tile(inv_freq, 4)
    ropec[:, 1] = np.tile(np.concatenate([-np.ones(32), np.ones(32)]), 2)
    return {
        "w_ada": np.stack([_wl(f("w_ada")[l]) for l in range(L)]),
        "pvec": pvec, "hvec": hvec, "fnorm": _pk(f("final_norm")),
        "w_in": np.stack(w_in), "w_uq": np.stack(w_uq), "w_ukv": np.stack(w_ukv),
        "w_out": np.stack([_wl(f("w_out")[l]) for l in range(L)]),
        "w_up": np.stack(w_up),
        "w_down": np.stack([_wl(f("w_down")[l]) for l in range(L)]),
        "consts": consts, "ropec": ropec,
    }


def _core_inputs(inp, shared, core, nb=NBC):
    b0 = core * NBC
    c = np.asarray(inp["c"], np.float32)[b0:b0 + NBC]
    cT = np.ascontiguousarray(c.T.reshape(8, 128, NBC).transpose(1, 0, 2))
    m = dict(shared)
    m["x"] = np.ascontiguousarray(np.asarray(inp["x"], np.float32)[b0:b0 + nb])
    m["cT"] = cT
    m["pos"] = np.ascontiguousarray(np.asarray(inp["positions"], np.int32)[b0:b0 + nb])
    return m


_NC_CACHE = {}


def kernel(**inputs):
    shared = _prep_shared(inputs)
    if "nc" not in _NC_CACHE:
        _NC_CACHE["nc"] = Builder().nc
    nc = _NC_CACHE["nc"]
    in_maps = [_core_inputs(inputs, shared, core) for core in range(NCORES)]
    res = run_bass_kernel_spmd(nc, in_maps, core_ids=list(range(NCORES)))
    out = np.concatenate([np.asarray(r["out"], np.float32) for r in res.results], axis=0)
    return out
```
